# Optimizing a Trainium2 kernel written in Bass

```python
import jax, jax.numpy as jnp
from jax import lax
import numpy as np


D_MODEL = 1024
BATCH = 8
SEQ = 2048
DEPTH = 2

CHUNK = 64
PLE_DIM = 256
EPS = 1e-6

HG_HEADS = 4
HG_DK = 128
HG_DV = 128
HG_WIDTH = HG_HEADS * HG_DK

CONV_WIDTH = 512
CONV_K = 31

D_FF = 2816
FFN_CONV_K = 3

COL_Q = HG_WIDTH
COL_F = HG_WIDTH
COL_I = HG_HEADS * HG_DV
COL_OG = HG_HEADS * HG_DV
COL_GLU = 2 * CONV_WIDTH
COL_GATE = 2 * D_MODEL
SPLITS = tuple(int(v) for v in np.cumsum([COL_Q, COL_F, COL_I, COL_OG, COL_GLU, D_MODEL]))
IN_COLS = COL_Q + COL_F + COL_I + COL_OG + COL_GLU + COL_GATE

kernel_name = "hybrid_hgrn2_conformer_convffn_ple"


def rmsnorm(x, g):
    xf = x.astype(jnp.float32)
    y = xf * lax.rsqrt(jnp.mean(xf * xf, axis=-1, keepdims=True) + EPS)
    return (y * g.astype(jnp.float32)).astype(x.dtype)


def layernorm(x, g, b):
    xf = x.astype(jnp.float32)
    mu = jnp.mean(xf, axis=-1, keepdims=True)
    var = jnp.mean(jnp.square(xf - mu), axis=-1, keepdims=True)
    y = (xf - mu) * lax.rsqrt(var + EPS)
    return (y * g.astype(jnp.float32) + b.astype(jnp.float32)).astype(x.dtype)


def causal_dwconv(x, w, b):
    k, c = w.shape
    y = lax.conv_general_dilated(
        x, w[:, None, :].astype(x.dtype), window_strides=(1,), padding=[(k - 1, 0)],
        dimension_numbers=("NWC", "WIO", "NWC"), feature_group_count=c)
    return y + b.astype(x.dtype)


def hgrn2_recurrence(q, k, v, log_f):
    bsz, seq, h, dk = q.shape
    dv = v.shape[-1]
    nc = seq // CHUNK

    def to_chunks(t):
        return t.astype(jnp.float32).reshape(bsz, nc, CHUNK, h, t.shape[-1]).transpose(1, 0, 3, 2, 4)

    qc, kc, vc, gc = to_chunks(q), to_chunks(k), to_chunks(v), to_chunks(log_f)
    mask = jnp.tril(jnp.ones((CHUNK, CHUNK), dtype=bool))[:, :, None]

    def step(state, inp):
        qb, kb, vb, gb = inp
        cum = jnp.cumsum(gb, axis=-2)
        o_inter = jnp.einsum("bhtk,bhkv->bhtv", qb * jnp.exp(cum), state)
        diff = cum[:, :, :, None, :] - cum[:, :, None, :, :]
        decay = jnp.exp(jnp.where(mask, diff, -jnp.inf))
        scores = jnp.einsum("bhtk,bhsk,bhtsk->bhts", qb, kb, decay)
        o_intra = jnp.einsum("bhts,bhsv->bhtv", scores, vb)
        last = cum[:, :, -1:, :]
        new_state = (jnp.exp(last[:, :, 0, :])[..., None] * state
                     + jnp.einsum("bhsk,bhsv->bhkv", kb * jnp.exp(last - cum), vb))
        return new_state, o_inter + o_intra

    s0 = jnp.zeros((bsz, h, dk, dv), jnp.float32)
    _, out = lax.scan(step, s0, (qc, kc, vc, gc))
    return out.transpose(1, 0, 3, 2, 4).reshape(bsz, seq, h, dv)


def setup_inputs(seed: int = 0) -> dict:
    key = jax.random.key(seed)
    ks = iter(jax.random.split(key, 32))
    nrm = lambda shape, scale: jax.random.normal(next(ks), shape, jnp.float32) * scale
    gain = lambda shape: 1.0 + nrm(shape, 0.01)
    return {
        "x": nrm((BATCH, SEQ, D_MODEL), 1.0),
        "p": nrm((DEPTH, BATCH, SEQ, PLE_DIM), 1.0),
        "g_mix": gain((DEPTH, D_MODEL)),
        "w_in": nrm((DEPTH, D_MODEL, IN_COLS), D_MODEL ** -0.5),
        "hg_lb_logits": nrm((DEPTH, HG_WIDTH), 0.5),
        "hg_norm_g": gain((DEPTH, HG_DV)),
        "w_br_a": nrm((DEPTH, HG_HEADS * HG_DV, D_MODEL), (HG_HEADS * HG_DV) ** -0.5),
        "b_glu": nrm((DEPTH, COL_GLU), 0.01),
        "conv_w": nrm((DEPTH, CONV_K, CONV_WIDTH), CONV_K ** -0.5),
        "conv_b": nrm((DEPTH, CONV_WIDTH), 0.01),
        "ln_g": gain((DEPTH, CONV_WIDTH)),
        "ln_b": nrm((DEPTH, CONV_WIDTH), 0.01),
        "w_br_b": nrm((DEPTH, CONV_WIDTH, D_MODEL), CONV_WIDTH ** -0.5),
        "w_out": nrm((DEPTH, D_MODEL, D_MODEL), D_MODEL ** -0.5),
        "g_ffn": gain((DEPTH, D_MODEL)),
        "w_up": nrm((DEPTH, D_MODEL, 2 * D_FF), D_MODEL ** -0.5),
        "ffn_conv_w": nrm((DEPTH, FFN_CONV_K, 2 * D_FF), FFN_CONV_K ** -0.5),
        "ffn_conv_b": nrm((DEPTH, 2 * D_FF), 0.01),
        "w_down": nrm((DEPTH, D_FF, D_MODEL), D_FF ** -0.5),
        "g_ple": gain((DEPTH, D_MODEL)),
        "w_ple_gate": nrm((DEPTH, D_MODEL, D_MODEL), D_MODEL ** -0.5),
        "w_ple_proj": nrm((DEPTH, PLE_DIM, D_MODEL), PLE_DIM ** -0.5),
        "g_final": gain((D_MODEL,)),
    }


def reference(x, p, g_mix, w_in, hg_lb_logits, hg_norm_g, w_br_a, b_glu, conv_w, conv_b,
              ln_g, ln_b, w_br_b, w_out, g_ffn, w_up, ffn_conv_w, ffn_conv_b, w_down,
              g_ple, w_ple_gate, w_ple_proj, g_final):
    bsz, seq, _ = x.shape
    lb_all = jnp.cumsum(jax.nn.softmax(hg_lb_logits.astype(jnp.float32), axis=0), axis=0)
    lb_all = lb_all - lb_all[0:1]

    for i in range(DEPTH):
        h = rmsnorm(x, g_mix[i])
        proj = h @ w_in[i]
        zq, zf, zi, zog, zglu, zga, zgb = jnp.split(proj, SPLITS, axis=-1)

        lb = lb_all[i]
        log_f = jnp.logaddexp(jnp.log(lb), jnp.log1p(-lb) + jax.nn.log_sigmoid(zf.astype(jnp.float32)))
        k_in = -jnp.expm1(log_f)
        q = jax.nn.silu(zq.astype(jnp.float32))
        o = hgrn2_recurrence(q.reshape(bsz, seq, HG_HEADS, HG_DK),
                             k_in.reshape(bsz, seq, HG_HEADS, HG_DK),
                             zi.reshape(bsz, seq, HG_HEADS, HG_DV),
                             log_f.reshape(bsz, seq, HG_HEADS, HG_DK))
        o = rmsnorm(o, hg_norm_g[i]).reshape(bsz, seq, HG_HEADS * HG_DV)
        o = (o * jax.nn.silu(zog.astype(jnp.float32))).astype(x.dtype)
        y_a = o @ w_br_a[i]

        u = zglu + b_glu[i]
        u = u[..., :CONV_WIDTH] * jax.nn.sigmoid(u[..., CONV_WIDTH:])
        u = causal_dwconv(u, conv_w[i], conv_b[i])
        u = jax.nn.silu(layernorm(u, ln_g[i], ln_b[i]))
        y_b = u @ w_br_b[i]

        y = jax.nn.sigmoid(zga) * y_a + jax.nn.sigmoid(zgb) * y_b
        x = x + y @ w_out[i]

        hf = rmsnorm(x, g_ffn[i])
        up = causal_dwconv(hf @ w_up[i], ffn_conv_w[i], ffn_conv_b[i])
        x = x + (jax.nn.silu(up[..., :D_FF]) * up[..., D_FF:]) @ w_down[i]

        gate = jax.nn.sigmoid(rmsnorm(x, g_ple[i]) @ w_ple_gate[i])
        x = x + gate * (p[i] @ w_ple_proj[i])

    return rmsnorm(x, g_final)
```

```python
from contextlib import ExitStack
import numpy as np
import concourse.bass as bass
import concourse.mybir as mybir
from concourse.bass_utils import run_bass_kernel_spmd

F32 = mybir.dt.float32
BF16 = mybir.dt.bfloat16
AF = mybir.ActivationFunctionType
ALU = mybir.AluOpType

D = 1024
SEQ = 2048
T = 1024
NHALF = 2
DEPTH = 2
EPS = 1e-6
DFF = 2816
NCOL = 368
C_GMIX, C_GFFN, C_GPLE, C_GFIN, C_HGN, C_BGLU, C_CW, C_CB, C_LNG, C_LNB, C_FW, C_FB, C_LG0, C_LG1 = (
    0, 8, 16, 24, 32, 33, 41, 165, 169, 173, 177, 309, 353, 357)
NCONST = 128 + 128 + 512
ARENA = 77824
NSLOT = 5
NORM_POOL_KC = (2, 5, 7)

DEBUG_STOP = None
DEBUG_SUB = None


class Tl:
    __slots__ = ("ap", "w", "r", "dsem", "dcnt", "name", "dead")

    def __init__(self, ap, name=""):
        self.ap = ap
        self.w = None
        self.r = {}
        self.dsem = None
        self.dcnt = 0
        self.name = name
        self.dead = False


class Prog:
    ENG = ("pe", "act", "dve", "pool", "sp")

    def __init__(self, nc, es):
        self.nc = nc
        self.es = es
        self.q = {e: [] for e in self.ENG}
        self.cnt = {e: 0 for e in self.ENG}
        self.sem = {e: es.enter_context(nc.semaphore("s_" + e)) for e in self.ENG}
        self.seen = {e: {} for e in self.ENG}
        self.nsem = 0

    def new_sem(self):
        self.nsem += 1
        return self.es.enter_context(self.nc.semaphore("d%d" % self.nsem))

    def _wait(self, eng, deps):
        for k, v in deps.items():
            if self.seen[eng].get(k, 0) >= v:
                continue
            self.seen[eng][k] = v
            sem = self.sem[k] if isinstance(k, str) else k
            self.q[eng].append(lambda e, sem=sem, v=v: e.wait_ge(sem, v))

    def _deps(self, eng, reads, writes):
        deps = {}

        def add(k, v):
            if v > deps.get(k, 0):
                deps[k] = v

        for t in reads:
            assert not t.dead, t.name
            if t.w:
                add(*t.w)
        for t in writes:
            assert not t.dead, t.name
            if t.w:
                add(*t.w)
            for k, v in t.r.items():
                add(k, v)
        if eng == "pe":
            deps.pop("pe", None)
        return deps

    def op(self, eng, fn, reads=(), writes=(), inc=True):
        self._wait(eng, self._deps(eng, reads, writes))
        if inc:
            self.cnt[eng] += 1
            c = self.cnt[eng]
            sem = self.sem[eng]
            self.q[eng].append(lambda e, fn=fn, sem=sem: fn(e).then_inc(sem, 1))
        else:
            c = self.cnt[eng] + 1
            self.q[eng].append(fn)
        for t in reads:
            if t.r.get(eng, 0) < c:
                t.r[eng] = c
        for t in writes:
            t.w = (eng, c)
            t.r = {}

    def dma(self, eng, fn, out_t=None, in_t=None, sem_t=None):
        reads = [in_t] if in_t is not None else []
        writes = [out_t] if out_t is not None else []
        self._wait(eng, self._deps(eng, reads, writes))
        st = out_t if out_t is not None else sem_t
        if st.dsem is None:
            st.dsem = self.new_sem()
        st.dcnt += 16
        sem, v = st.dsem, st.dcnt
        self.q[eng].append(lambda e, fn=fn, sem=sem: fn(e).then_inc(sem, 16))
        if out_t is not None:
            out_t.w = (sem, v)
            out_t.r = {}
        if in_t is not None:
            in_t.r[sem] = v

    def run(self, block):
        q = self.q

        @block.tensor
        def _(e):
            for f in q["pe"]:
                f(e)

        @block.scalar
        def _(e):
            for f in q["act"]:
                f(e)

        @block.vector
        def _(e):
            for f in q["dve"]:
                f(e)

        @block.gpsimd
        def _(e):
            for f in q["pool"]:
                f(e)

        @block.sync
        def _(e):
            for f in q["sp"]:
                f(e)


def build_nc(n_layers=DEPTH, stop=None, sub=None):
    nc = bass.Bass("TRN2", target_bir_lowering=False)
    dr = {}

    def din(name, shape):
        dr[name] = nc.dram_tensor(name, list(shape), F32, kind="ExternalInput").ap()
        return dr[name]

    xT = din("xT", [D, SEQ])
    pT = din("pT", [DEPTH, 256, SEQ])
    cvec_d = din("cvec", [DEPTH, 128, NCOL])
    const_d = din("consts", [128, NCONST])
    w_in = din("w_in", [DEPTH, D, 5120])
    w_br_a = din("w_br_a", [DEPTH, 512, D])
    w_br_b = din("w_br_b", [DEPTH, 512, D])
    w_out = din("w_out", [DEPTH, D, D])
    w_up = din("w_up", [DEPTH, D, 2 * DFF])
    w_down = din("w_down", [DEPTH, DFF, D])
    w_pg = din("w_ple_gate", [DEPTH, D, D])
    w_pp = din("w_ple_proj", [DEPTH, 256, D])
    outT = nc.dram_tensor("outT", [D, SEQ], F32, kind="ExternalOutput").ap()

    with ExitStack() as es:
        x_sb = es.enter_context(nc.sbuf_tensor("x_sb", [128, 8, SEQ], F32))
        h_sb = es.enter_context(nc.sbuf_tensor("h_sb", [128, 8, T], BF16))
        ws_sb = es.enter_context(nc.sbuf_tensor("ws_sb", [128, NSLOT, 4096], BF16))
        ar = es.enter_context(nc.sbuf_tensor("arena", [128, ARENA // 2], BF16))
        cv_sb = es.enter_context(nc.sbuf_tensor("cv_sb", [128, DEPTH, NCOL], F32))
        ident_sb = es.enter_context(nc.sbuf_tensor("ident", [128, 128], BF16))
        ones_sb = es.enter_context(nc.sbuf_tensor("ones", [128, 128], BF16))
        mask_sb = es.enter_context(nc.sbuf_tensor("mask", [128, 128], F32))
        smask_sb = es.enter_context(nc.sbuf_tensor("smask", [128, 512], F32))
        S32_sb = es.enter_context(nc.sbuf_tensor("S32", [128, 512], F32))
        uhalo_sb = es.enter_context(nc.sbuf_tensor("uhalo", [128, 4, 32], BF16))
        stash_sb = es.enter_context(nc.sbuf_tensor("stash", [128, 44, 2], BF16))
        lb_sb = es.enter_context(nc.sbuf_tensor("lb", [128, 4, 4], F32))
        lbc_sb = es.enter_context(nc.sbuf_tensor("lbc", [128, 2, 4], F32))
        hsm_sb = es.enter_context(nc.sbuf_tensor("hsm", [128, 3, 32], F32))
        lbs_sb = es.enter_context(nc.sbuf_tensor("lbs", [128, 2, 2, 4], F32))
        pss = [es.enter_context(nc.psum_tensor("ps%d" % i, [128, 1024], F32)) for i in range(4)]
        P = Prog(nc, es)
        block = es.enter_context(nc.Block())

        xs = [[Tl(x_sb[:, kc, hf * T:(hf + 1) * T], "x%d_%d" % (kc, hf)) for hf in range(NHALF)] for kc in range(8)]
        h = [[Tl(h_sb[:, kc, t_ * 512:(t_ + 1) * 512], "h%d_%d" % (kc, t_)) for t_ in range(2)] for kc in range(8)]
        slots = [Tl(ws_sb[:, i, :], "slot%d" % i) for i in range(NSLOT)]
        cv = [Tl(cv_sb[:, l, :], "cv%d" % l) for l in range(DEPTH)]
        ident = Tl(ident_sb[:], "ident")
        ones = Tl(ones_sb[:], "ones")
        mask = Tl(mask_sb[:], "mask")
        smask = Tl(smask_sb[:], "smask")
        S32 = Tl(S32_sb[:], "S32")
        uhalo = Tl(uhalo_sb[:], "uhalo")
        stash = Tl(stash_sb[:], "stash")
        lbt = Tl(lb_sb[:], "lb")
        lbc = Tl(lbc_sb[:], "lbc")
        elc = Tl(hsm_sb[:, 0, :], "elc")
        emc = Tl(hsm_sb[:, 1, :], "emc")
        mm_ = Tl(hsm_sb[:, 2, :], "m")
        lbs = Tl(lbs_sb[:], "lbs")
        ps_live = []
        ps_state = {"mode": None, "tiles": [], "rr": 0}

        def PSTL(b0, nb):
            k = b0 // 2
            ap = pss[k][:, :] if nb == 2 else pss[k][:, (b0 % 2) * 512:(b0 % 2 + 1) * 512]
            t = Tl(ap, "psum%d_%d" % (b0, nb))
            keep = []
            for (o, e_, old) in ps_live:
                if o < b0 + nb and b0 < e_:
                    old.dead = True
                    for kk, val in old.r.items():
                        if t.r.get(kk, 0) < val:
                            t.r[kk] = val
                    if old.w:
                        kk, val = old.w
                        if t.r.get(kk, 0) < val:
                            t.r[kk] = val
                else:
                    keep.append((o, e_, old))
            ps_live[:] = keep
            ps_live.append((b0, b0 + nb, t))
            return t

        def ps_mode(mode):
            if ps_state["mode"] == mode:
                return
            ps_state["mode"] = mode
            ps_state["rr"] = 0
            if mode == "single":
                ps_state["tiles"] = [PSTL(i, 1) for i in range(8)]
            else:
                ps_state["tiles"] = [PSTL(2 * i, 2) for i in range(4)]

        def ps():
            ps_mode("single")
            t = ps_state["tiles"][ps_state["rr"] % 8]
            ps_state["rr"] += 1
            return t

        def ps2():
            ps_mode("pair")
            t = ps_state["tiles"][ps_state["rr"] % 4]
            ps_state["rr"] += 1
            return t

        live = []

        def AR(off, nbytes, dtype=BF16, name=""):
            assert off % 4 == 0 and nbytes % 4 == 0 and off + nbytes <= ARENA, (name, off, nbytes)
            v = ar[:, off // 2:(off + nbytes) // 2]
            if dtype == F32:
                v = v.bitcast(F32)
            t = Tl(v, name)
            keep = []
            for (o, e_, old) in live:
                if o < off + nbytes and off < e_:
                    old.dead = True
                    for k, val in old.r.items():
                        if t.r.get(k, 0) < val:
                            t.r[k] = val
                    if old.w:
                        k, val = old.w
                        if t.r.get(k, 0) < val:
                            t.r[k] = val
                else:
                    keep.append((o, e_, old))
            live[:] = keep
            live.append((off, off + nbytes, t))
            return t

        def mm(out_t, out_ap, l_t, l_ap, r_t, r_ap, start, stop, inc):
            P.op("pe", lambda e: e.matmul(out_ap, l_ap, r_ap, start=start, stop=stop),
                 reads=[l_t, r_t], writes=[out_t], inc=inc)

        def act(out_t, out_ap, in_t, in_ap, func, bias=None, scale=None, rd=()):
            kw = {}
            if bias is not None:
                kw["bias"] = bias
            if scale is not None:
                kw["scale"] = scale
            P.op("act", lambda e: e.activation(out=out_ap, in_=in_ap, func=func, **kw),
                 reads=[in_t] + list(rd), writes=[out_t])

        def tt(out_t, out_ap, a_t, a_ap, b_t, b_ap, op, eng="dve"):
            P.op(eng, lambda e: e.tensor_tensor(out=out_ap, in0=a_ap, in1=b_ap, op=op),
                 reads=[a_t, b_t], writes=[out_t])

        def stt(out_t, out_ap, a_t, a_ap, scalar, b_t, b_ap, op0, op1, rd=()):
            P.op("dve", lambda e: e.scalar_tensor_tensor(out=out_ap, in0=a_ap, scalar=scalar, in1=b_ap, op0=op0, op1=op1),
                 reads=[a_t, b_t] + list(rd), writes=[out_t])

        def ts(out_t, out_ap, a_t, a_ap, s1, s2, op0, op1=None, rd=(), eng="dve"):
            if op1 is None:
                P.op(eng, lambda e: e.tensor_scalar(out=out_ap, in0=a_ap, scalar1=s1, scalar2=None, op0=op0),
                     reads=[a_t] + list(rd), writes=[out_t])
            else:
                P.op(eng, lambda e: e.tensor_scalar(out=out_ap, in0=a_ap, scalar1=s1, scalar2=s2, op0=op0, op1=op1),
                     reads=[a_t] + list(rd), writes=[out_t])

        def cp(out_t, out_ap, in_t, in_ap, eng="dve"):
            P.op(eng, lambda e: e.tensor_copy(out=out_ap, in_=in_ap), reads=[in_t], writes=[out_t])

        def mset(t, ap, val, eng="dve"):
            P.op(eng, lambda e: e.memset(ap, val), writes=[t])

        blocks = []
        wstate = {"emitted": 0, "released": -1}

        def wadd(fns):
            blocks.append(fns)
            return len(blocks) - 1

        def w_emit_upto(n):
            while wstate["emitted"] < min(n + 1, len(blocks)):
                i = wstate["emitted"]
                st = slots[i % NSLOT]
                for fn in blocks[i]:
                    P.dma("pool", lambda e, fn=fn, st=st: fn(e, st.ap), out_t=st)
                wstate["emitted"] += 1

        def wtile(i):
            assert i < wstate["emitted"], (i, wstate)
            assert i > wstate["released"]
            return slots[i % NSLOT]

        def wrel(i):
            assert i == wstate["released"] + 1, (i, wstate)
            wstate["released"] = i
            w_emit_upto(i + NSLOT)

        def blk_k(src, l, k, c0, ncol):
            def fn(e, sap):
                return e.dma_start(out=sap[:, 0:k * ncol].rearrange("p (k n) -> p k n", k=k),
                                   in_=src[l, :, c0:c0 + ncol].rearrange("(k p) n -> p k n", p=128))
            return fn

        def blk_br(l, c0):
            def fa(e, sap):
                return e.dma_start(out=sap[:, 0:2048].rearrange("p (k n) -> p k n", k=4),
                                   in_=w_br_a[l, :, c0:c0 + 512].rearrange("(k p) n -> p k n", p=128))

            def fb(e, sap):
                return e.dma_start(out=sap[:, 2048:4096].rearrange("p (k n) -> p k n", k=4),
                                   in_=w_br_b[l, :, c0:c0 + 512].rearrange("(k p) n -> p k n", p=128))
            return [fa, fb]

        def blk_wd(l, kh, c0):
            def fn(e, sap):
                return e.dma_start(out=sap[:, 0:2816].rearrange("p (k n) -> p k n", k=11),
                                   in_=w_down[l, kh * 1408:(kh + 1) * 1408, c0:c0 + 256].rearrange("(k p) n -> p k n", p=128))
            return fn

        def plan_pass(l, stages):
            W = {}
            if "mix" in stages:
                for nm, b in (("f", 1), ("q", 0), ("v", 2), ("og", 3), ("glua", 4), ("glub", 5)):
                    W[nm] = wadd([blk_k(w_in, l, 8, b * 512, 512)])
                for ob in range(2):
                    W["ga%d" % ob] = wadd([blk_k(w_in, l, 8, 3072 + ob * 512, 512)])
                    W["gb%d" % ob] = wadd([blk_k(w_in, l, 8, 4096 + ob * 512, 512)])
                    W["br%d" % ob] = wadd(blk_br(l, ob * 512))
                for ob in range(2):
                    W["wo%d" % ob] = wadd([blk_k(w_out, l, 8, ob * 512, 512)])
            if "ffn" in stages:
                for g in range(6):
                    nco = 512 if g < 5 else 256
                    W["ug%d" % g] = wadd([blk_k(w_up, l, 8, g * 512, nco)])
                    W["uv%d" % g] = wadd([blk_k(w_up, l, 8, DFF + g * 512, nco)])
                for cb in range(4):
                    W["wd%d_0" % cb] = wadd([blk_wd(l, 0, cb * 256)])
                    W["wd%d_1" % cb] = wadd([blk_wd(l, 1, cb * 256)])
            if "ple" in stages:
                for ob in range(2):
                    W["pg%d" % ob] = wadd([blk_k(w_pg, l, 8, ob * 512, 512)])
                    W["pp%d" % ob] = wadd([blk_k(w_pp, l, 2, ob * 512, 512)])
            return W

        def stages_for(l):
            if stop is not None and l == stop[0]:
                return {"mix": ["mix"], "ffn": ["mix", "ffn"], "ple": ["mix", "ffn", "ple"]}[stop[1]]
            return ["mix", "ffn", "ple"]

        plans = []
        for l in range(n_layers):
            for hf in range(NHALF):
                plans.append(plan_pass(l, stages_for(l)))

        for l in range(DEPTH):
            P.dma("sp", lambda e, l=l: e.dma_start(out=cv[l].ap, in_=cvec_d[l, :, :]), out_t=cv[l])
        P.dma("sp", lambda e: e.dma_start(out=mask.ap, in_=const_d[:, 128:256]), out_t=mask)
        P.dma("sp", lambda e: e.dma_start(out=smask.ap, in_=const_d[:, 256:768]), out_t=smask)
        P.dma("pool", lambda e: e.dma_start(out=ident.ap, in_=const_d[:, 0:128]), out_t=ident)
        for hf in range(NHALF):
            for kc in range(8):
                P.dma("sp", lambda e, kc=kc, hf=hf: e.dma_start(out=xs[kc][hf].ap, in_=xT[kc * 128:(kc + 1) * 128, hf * T:(hf + 1) * T]),
                      out_t=xs[kc][hf])
        for kc in range(8):
            P._wait("pool", {xs[kc][0].w[0]: xs[kc][0].w[1]})
        w_emit_upto(1)
        mset(ones, ones.ap, 1.0)
        mset(lbc, lbc.ap[:, 0, :], 0.0)
        mset(lbc, lbc.ap[:, 1, :], 1.0)
        act(lbt, lbt.ap[:, 2, :], cv[0], cv[0].ap[:, C_LG0:C_LG0 + 4], AF.Exp)
        act(lbt, lbt.ap[:, 3, :], cv[0], cv[0].ap[:, C_LG1:C_LG1 + 4], AF.Exp)
        tt(lbt, lbt.ap[:, 0, :], lbt, lbt.ap[:, 2, :], lbt, lbt.ap[:, 3, :], ALU.add)
        P.op("dve", lambda e: e.reciprocal(out=lbt.ap[:, 0, :], in_=lbt.ap[:, 0, :]), reads=[lbt], writes=[lbt])
        tt(lbt, lbt.ap[:, 1, :], lbt, lbt.ap[:, 2, :], lbt, lbt.ap[:, 0, :], ALU.mult)
        tt(lbt, lbt.ap[:, 0, :], lbt, lbt.ap[:, 3, :], lbt, lbt.ap[:, 0, :], ALU.mult)

        ts(lbs, lbs.ap[:, 0, 0, :], lbc, lbc.ap[:, 1, :], 0.5, None, ALU.mult)
        tt(lbs, lbs.ap[:, 0, 1, :], lbs, lbs.ap[:, 0, 0, :], lbc, lbc.ap[:, 0, :], ALU.add)
        ts(lbs, lbs.ap[:, 1, 0, :], lbt, lbt.ap[:, 1, :], 0.5, None, ALU.mult)
        tt(lbs, lbs.ap[:, 1, 1, :], lbs, lbs.ap[:, 1, 0, :], lbt, lbt.ap[:, 0, :], ALU.add)

        def lb_ap(l, hd):
            return lbs.ap[:, l, 1, hd:hd + 1], lbs.ap[:, l, 0, hd:hd + 1], lbs

        NSCR = 57344

        def rmsnorm_h(l, hf, gcol):
            nsq = [AR(NSCR + i * 1024, 1024, BF16, "nsq%d" % i) for i in range(4)]
            nrs = [AR(NSCR + 4096 + i * 2048, 2048, F32, "nrs%d" % i) for i in range(2)]
            ssp = ps2()
            sshalf = [Tl(ssp.ap[:, i * 512:(i + 1) * 512], "ssh%d" % i) for i in range(2)]
            for hlf in sshalf:
                hlf.r = dict(ssp.r)
                hlf.w = ssp.w
            for t_ in range(2):
                sl = slice(t_ * 512, (t_ + 1) * 512)
                ss = sshalf[t_]
                for kc in range(8):
                    sq = nsq[kc % 4]
                    if kc % 2 == 0:
                        act(sq, sq.ap, xs[kc][hf], xs[kc][hf].ap[:, sl], AF.Square)
                    else:
                        tt(sq, sq.ap, xs[kc][hf], xs[kc][hf].ap[:, sl], xs[kc][hf], xs[kc][hf].ap[:, sl], ALU.mult)
                    mm(ss, ss.ap, ones, ones.ap, sq, sq.ap, kc == 0, kc == 7, True)
                sd = nrs[t_]
                act(sd, sd.ap, ss, ss.ap, AF.Ln, bias=EPS, scale=1.0 / D)
                act(sd, sd.ap, sd, sd.ap, AF.Exp, scale=-0.5)
                for kc in range(8):
                    stt(h[kc][t_], h[kc][t_].ap, xs[kc][hf], xs[kc][hf].ap[:, sl], cv[l].ap[:, gcol + kc:gcol + kc + 1],
                        sd, sd.ap, ALU.mult, ALU.mult, rd=[cv[l]])
            for hlf in sshalf:
                for kk, val in list(hlf.r.items()) + ([hlf.w] if hlf.w else []):
                    if ssp.r.get(kk, 0) < val:
                        ssp.r[kk] = val

        def kview(st, k, ncol):
            return st.ap[:, 0:k * ncol].rearrange("p (k n) -> p k n", k=k)

        def mixer(l, hf, W):
            cvl = cv[l]
            og_all = AR(0, 8192, BF16, "og")
            ogv = og_all.ap.rearrange("p (h t) -> p h t", h=4)
            rmsnorm_h(l, hf, C_GMIX)
            w_emit_upto(wstate["released"] + NSLOT)
            if sub == 'norm':
                return
            QT = [AR(16384 + hd * 2048, 2048, BF16, "QT%d" % hd) for hd in range(4)]
            KT = [AR(24576 + hd * 2048, 2048, BF16, "KT%d" % hd) for hd in range(4)]
            Vtok = [AR(32768 + j * 1024, 1024, BF16, "V%d" % j) for j in range(8)]
            sogX = AR(40960, 8192, BF16, "sogX")
            sogv = sogX.ap.rearrange("p (c h t) -> p c h t", c=8, h=4)
            Ktok = [AR(49152 + i * 1024, 1024, BF16, "Ktok%d" % i) for i in range(2)]
            Sb2 = [AR(51200 + i * 1024, 1024, BF16, "Sb%d" % i) for i in range(2)]
            smT = [AR(53248 + i * 1024, 1024, BF16, "smT%d" % i) for i in range(2)]
            Sp32 = AR(55296, 2048, F32, "Sp32")
            for i in range(2):
                mset(smT[i], smT[i].ap.rearrange("p (h v) -> p h v", h=4)[64:128, :, 0:64], 0.0)
            tmp = [[AR(57344 + (s * 5 + i) * 2048, 2048, F32, "tmp%d_%d" % (s, i)) for i in range(5)] for s in range(2)]
            elc3 = elc.ap.rearrange("p (h c) -> p h c", h=4)
            emc3 = emc.ap.rearrange("p (h c) -> p h c", h=4)
            m3 = mm_.ap.rearrange("p (h c) -> p h c", h=4)

            if hf == 0:
                mset(S32, S32.ap, 0.0)
            wf, wq = wtile(W["f"]), wtile(W["q"])
            wfv, wqv = kview(wf, 8, 512), kview(wq, 8, 512)
            wv, wg = wtile(W["v"]), wtile(W["og"])
            wvv, wgv = kview(wv, 8, 512), kview(wg, 8, 512)
            for hd in range(4):
                lb_a, oml_a, lb_t = lb_ap(l, hd)
                for t_ in range(2):
                    sl = slice(t_ * 512, (t_ + 1) * 512)
                    A, G, B, C, Fq = tmp[(hd * 2 + t_) % 2]
                    pf = ps()
                    for kc in range(8):
                        mm(pf, pf.ap, wf, wfv[:, kc, hd * 128:(hd + 1) * 128], h[kc][t_], h[kc][t_].ap, kc == 0, kc == 7, kc == 7)
                    pq = ps()
                    for kc in range(8):
                        mm(pq, pq.ap, wq, wqv[:, kc, hd * 128:(hd + 1) * 128], h[kc][t_], h[kc][t_].ap, kc == 0, kc == 7, kc == 7)
                    j = hd * 2 + t_
                    pz = ps()
                    for kc in range(8):
                        mm(pz, pz.ap, wg, wgv[:, kc, hd * 128:(hd + 1) * 128], h[kc][t_], h[kc][t_].ap, kc == 0, kc == 7, kc == 7)
                    pv = ps()
                    for kc in range(8):
                        mm(pv, pv.ap, h[kc][j // 4], h[kc][j // 4].ap[:, (j % 4) * 128:(j % 4 + 1) * 128], wv, wvv[:, kc, :], kc == 0, kc == 7, kc == 7)
                    act(A, A.ap, pf, pf.ap, AF.Tanh, scale=0.5)
                    act(Fq, Fq.ap, pq, pq.ap, AF.Silu)
                    act(sogX, sogv[:, t_ * 4:(t_ + 1) * 4, hd, :], pz, pz.ap.rearrange("p (c t) -> p c t", c=4), AF.Silu)
                    ts(A, A.ap, A, A.ap, oml_a, lb_a, ALU.mult, ALU.add, rd=[lb_t])
                    cp(Vtok[j], Vtok[j].ap, pv, pv.ap)
                    ts(G, G.ap, A, A.ap, -1.0, 1.0, ALU.mult, ALU.add)
                    act(B, B.ap, A, A.ap, AF.Ln)
                    P.op("dve", lambda e, C=C, B=B: e.tensor_tensor_scan(out=C.ap, data0=smask.ap, data1=B.ap, initial=0.0,
                                                                         op0=ALU.mult, op1=ALU.add),
                         reads=[smask, B], writes=[C])
                    c0 = t_ * 4
                    cp(mm_, m3[:, hd, c0:c0 + 4], C, C.ap[:, 63::128])
                    act(emc, emc3[:, hd, c0:c0 + 4], mm_, m3[:, hd, c0:c0 + 4], AF.Exp)
                    C3 = C.ap.rearrange("p (c t) -> p c t", c=4)
                    tt(C, C3, C, C3, mm_, m3[:, hd, c0:c0 + 4].unsqueeze(2).to_broadcast([128, 4, 128]), ALU.subtract)
                    act(B, B.ap, C, C.ap, AF.Exp)
                    act(A, A.ap, C, C.ap, AF.Exp, scale=-1.0)
                    cp(elc, elc3[:, hd, c0:c0 + 4], B, B.ap[:, 127::128])
                    tt(QT[hd], QT[hd].ap[:, sl], Fq, Fq.ap, B, B.ap, ALU.mult)
                    tt(KT[hd], KT[hd].ap[:, sl], G, G.ap, A, A.ap, ALU.mult)
            wrel(W["f"])
            wrel(W["q"])
            wrel(W["v"])
            wrel(W["og"])
            UW = 1056
            u = [AR(67584 + cc * UW * 2, UW * 2, BF16, "u%d" % cc) for cc in range(4)]
            sgt = [AR(8192 + i * 2048, 2048, F32, "sgt%d" % i) for i in range(2)]
            wa, wb = wtile(W["glua"]), wtile(W["glub"])
            wav, wbv = kview(wa, 8, 512), kview(wb, 8, 512)
            for cc in range(4):
                if hf == 0:
                    mset(u[cc], u[cc].ap[:, 0:30], 0.0)
                else:
                    cp(u[cc], u[cc].ap[:, 0:30], uhalo, uhalo.ap[:, cc, 0:30])
                for t_ in range(2):
                    sl = slice(t_ * 512, (t_ + 1) * 512)
                    pa = ps()
                    for kc in range(8):
                        mm(pa, pa.ap, wa, wav[:, kc, cc * 128:(cc + 1) * 128], h[kc][t_], h[kc][t_].ap, kc == 0, kc == 7, kc == 7)
                    pg = ps()
                    for kc in range(8):
                        mm(pg, pg.ap, wb, wbv[:, kc, cc * 128:(cc + 1) * 128], h[kc][t_], h[kc][t_].ap, kc == 0, kc == 7, kc == 7)
                    sg = sgt[(cc * 2 + t_) % 2]
                    act(sg, sg.ap, pg, pg.ap, AF.Sigmoid, bias=cvl.ap[:, C_BGLU + 4 + cc:C_BGLU + 5 + cc], rd=[cvl])
                    stt(u[cc], u[cc].ap[:, 30 + t_ * 512:30 + (t_ + 1) * 512], pa, pa.ap, cvl.ap[:, C_BGLU + cc:C_BGLU + cc + 1],
                        sg, sg.ap, ALU.add, ALU.mult, rd=[cvl])
                if hf == 0:
                    cp(uhalo, uhalo.ap[:, cc, 0:30], u[cc], u[cc].ap[:, T:T + 30])
            wrel(W["glua"])
            wrel(W["glub"])
            osq_ = [AR(57344 + i * 5120, 1024, BF16, "osq%d" % i) for i in range(2)]
            osd_ = [AR(57344 + i * 5120 + 1024, 2048, F32, "osd%d" % i) for i in range(2)]
            ot32_ = [AR(57344 + i * 5120 + 3072, 2048, F32, "ot32%d" % i) for i in range(2)]
            ps_state["mode"] = None
            ps_mode("single")
            pb = ps_state["tiles"]
            p_o = [pb[0], pb[1]]
            p_scs = [pb[2], pb[3]]
            p_kvs = [pb[4], pb[5]]
            p_ss, p_tr = pb[6], pb[7]
            p_trb = p_tr.ap.bitcast(BF16)
            S3v = S32.ap.rearrange("p (h v) -> p h v", h=4)
            Sp3v = Sp32.ap.rearrange("p (h v) -> p h v", h=4)

            def hg_pre(c):
                cs = slice(c * 128, (c + 1) * 128)
                kt, sm, p_sc, p_kv = Ktok[c % 2], smT[c % 2], p_scs[c % 2], p_kvs[c % 2]
                for hd in range(4):
                    P.op("pe", lambda e, hd=hd, cs=cs: e.transpose(p_trb[:, hd * 128:(hd + 1) * 128], KT[hd].ap[:, cs], ident.ap),
                         reads=[KT[hd], ident], writes=[p_tr], inc=(hd == 3))
                act(kt, kt.ap, p_tr, p_trb[:, 0:512], AF.Copy)
                for hd in range(4):
                    b0 = hd * 128
                    t0_ = c * 128
                    mm(p_sc, p_sc.ap[:, b0 + 64:b0 + 128], KT[hd], KT[hd].ap[:, cs], QT[hd], QT[hd].ap[:, t0_ + 64:t0_ + 128],
                       True, True, False)
                    mm(p_sc, p_sc.ap[0:64, b0:b0 + 64], KT[hd], KT[hd].ap[:, t0_:t0_ + 64], QT[hd], QT[hd].ap[:, t0_:t0_ + 64],
                       True, True, hd == 3)
                sm3 = sm.ap.rearrange("p (h v) -> p h v", h=4)
                sc3 = p_sc.ap.rearrange("p (h v) -> p h v", h=4)
                tt(sm, sm3[:, :, 64:128], p_sc, sc3[:, :, 64:128],
                   mask, mask.ap[:, 64:128].unsqueeze(1).to_broadcast([128, 4, 64]), ALU.mult)
                tt(sm, sm3[0:64, :, 0:64], p_sc, sc3[0:64, :, 0:64],
                   mask, mask.ap[0:64, 0:64].unsqueeze(1).to_broadcast([64, 4, 64]), ALU.mult)
                for hd in range(4):
                    hs = slice(hd * 128, (hd + 1) * 128)
                    mm(p_kv, p_kv.ap[:, hs], kt, kt.ap[:, hs], Vtok[c], Vtok[c].ap[:, hs], True, True, hd == 3)

            def hg_rec(c):
                cs = slice(c * 128, (c + 1) * 128)
                sm, p_kv, po, Sb = smT[c % 2], p_kvs[c % 2], p_o[c % 2], Sb2[c % 2]
                tt(Sp32, Sp3v, S32, S3v, emc, emc3[:, :, c:c + 1].to_broadcast([128, 4, 128]), ALU.mult)
                act(Sb, Sb.ap, Sp32, Sp32.ap, AF.Copy)
                tt(S32, S32.ap, p_kv, p_kv.ap, Sp32, Sp32.ap, ALU.add)
                tt(S32, S3v, S32, S3v, elc, elc3[:, :, c:c + 1].to_broadcast([128, 4, 128]), ALU.mult)
                for hd in range(4):
                    hs = slice(hd * 128, (hd + 1) * 128)
                    mm(po, po.ap[:, hs], Vtok[c], Vtok[c].ap[:, hs], sm, sm.ap[:, hs], True, False, False)
                    mm(po, po.ap[:, hs], Sb, Sb.ap[:, hs], QT[hd], QT[hd].ap[:, cs], False, True, hd == 3)
                osq, osd, ot32 = osq_[c % 2], osd_[c % 2], ot32_[c % 2]
                act(osq, osq.ap, po, po.ap, AF.Square)
                mm(p_ss, p_ss.ap, ones, ones.ap, osq, osq.ap, True, True, True)
                act(osd, osd.ap, p_ss, p_ss.ap, AF.Ln, bias=EPS, scale=1.0 / 128)
                act(osd, osd.ap, osd, osd.ap, AF.Exp, scale=-0.5)

            def hg_post(c):
                cs = slice(c * 128, (c + 1) * 128)
                po, osd, ot32 = p_o[c % 2], osd_[c % 2], ot32_[c % 2]
                stt(ot32, ot32.ap, po, po.ap, cvl.ap[:, C_HGN:C_HGN + 1], osd, osd.ap, ALU.mult, ALU.mult, rd=[cvl])
                tt(og_all, ogv[:, :, cs], ot32, ot32.ap.rearrange("p (h t) -> p h t", h=4), sogX, sogv[:, c, :, :], ALU.mult)

            hg_pre(0)
            for c in range(8):
                if c + 1 < 8:
                    hg_pre(c + 1)
                hg_rec(c)
                if c >= 1:
                    hg_post(c - 1)
            hg_post(7)

            if sub == 'hgrn':
                return
            dg = [AR(24832 + i * 7936, 7936, BF16, "dg%d" % i) for i in range(2)]
            uc32 = [AR(40704 + cc * 4096, 4096, F32, "uc32_%d" % cc) for cc in range(4)]

            dgt = [[Tl(dg[i].ap[:, k * 128:(k + 1) * 128], "dgt%d_%d" % (i, k)) for k in range(31)] for i in range(2)]
            for i in range(2):
                for k in range(31):
                    dgt[i][k].r = dict(dg[i].r)

            def build_dg(cc):
                for k in range(31):
                    tk = dgt[cc % 2][k]
                    ts(tk, tk.ap, ident, ident.ap, cvl.ap[:, C_CW + cc * 31 + k:C_CW + cc * 31 + k + 1], None, ALU.mult, rd=[cvl])

            build_dg(0)
            build_dg(1)
            for cc in range(4):
                for t_ in range(2):
                    sl = slice(t_ * 512, (t_ + 1) * 512)
                    pc = ps()
                    for k in range(31):
                        tk = dgt[cc % 2][k]
                        mm(pc, pc.ap, tk, tk.ap, u[cc], u[cc].ap[:, t_ * 512 + k:t_ * 512 + k + 512], k == 0, k == 30, k == 30)
                    act(uc32[cc], uc32[cc].ap[:, sl], pc, pc.ap, AF.Identity, bias=cvl.ap[:, C_CB + cc:C_CB + cc + 1], rd=[cvl])
                if cc + 2 < 4:
                    build_dg(cc + 2)
            for i in range(2):
                for tk in dgt[i]:
                    for kk, val in list(tk.r.items()) + ([tk.w] if tk.w else []):
                        if dg[i].r.get(kk, 0) < val:
                            dg[i].r[kk] = val
            ucs = [AR(8192 + cc * 2048, 2048, BF16, "ucs%d" % cc) for cc in range(4)]
            ucb2 = [[AR(57088 + t_ * 8192 + cc * 1024, 1024, BF16, "ucb%d_%d" % (t_, cc)) for cc in range(4)] for t_ in range(2)]
            sqb2 = [[AR(61184 + t_ * 8192 + cc * 1024, 1024, BF16, "sqb%d_%d" % (t_, cc)) for cc in range(4)] for t_ in range(2)]
            mt2 = [AR(16384 + t_ * 4096, 2048, F32, "mt%d" % t_) for t_ in range(2)]
            vt2 = [AR(18432 + t_ * 4096, 2048, F32, "vt%d" % t_) for t_ in range(2)]
            lnp = []
            for t_ in range(2):
                sl = slice(t_ * 512, (t_ + 1) * 512)
                ucb, sqb = ucb2[t_], sqb2[t_]
                for cc in range(4):
                    cp(ucb[cc], ucb[cc].ap, uc32[cc], uc32[cc].ap[:, sl])
                    act(sqb[cc], sqb[cc].ap, uc32[cc], uc32[cc].ap[:, sl], AF.Square)
                ps_mode("single")
                p1 = ps_state["tiles"][6 - 2 * t_]
                for cc in range(4):
                    mm(p1, p1.ap, ones, ones.ap, ucb[cc], ucb[cc].ap, cc == 0, cc == 3, cc == 3)
                p2 = ps_state["tiles"][7 - 2 * t_]
                for cc in range(4):
                    mm(p2, p2.ap, ones, ones.ap, sqb[cc], sqb[cc].ap, cc == 0, cc == 3, cc == 3)
                lnp.append((p1, p2))
            for t_ in range(2):
                sl = slice(t_ * 512, (t_ + 1) * 512)
                p1, p2 = lnp[t_]
                mt, vt = mt2[t_], vt2[t_]
                act(mt, mt.ap, p1, p1.ap, AF.Identity, scale=1.0 / 512)
                tt(vt, vt.ap, mt, mt.ap, mt, mt.ap, ALU.mult)
                stt(vt, vt.ap, p2, p2.ap, 1.0 / 512, vt, vt.ap, ALU.mult, ALU.subtract)
                act(vt, vt.ap, vt, vt.ap, AF.Ln, bias=EPS)
                act(vt, vt.ap, vt, vt.ap, AF.Exp, scale=-0.5)
                for cc in range(4):
                    tt(uc32[cc], uc32[cc].ap[:, sl], uc32[cc], uc32[cc].ap[:, sl], mt, mt.ap, ALU.subtract)
                    tt(uc32[cc], uc32[cc].ap[:, sl], uc32[cc], uc32[cc].ap[:, sl], vt, vt.ap, ALU.mult)
                    act(ucs[cc], ucs[cc].ap[:, sl], uc32[cc], uc32[cc].ap[:, sl], AF.Silu,
                        bias=cvl.ap[:, C_LNB + cc:C_LNB + cc + 1], scale=cvl.ap[:, C_LNG + cc:C_LNG + cc + 1], rd=[cvl])

            if sub == 'conv':
                return
            y = [AR(16384 + oc * 2048, 2048, BF16, "y%d" % oc) for oc in range(8)]
            gt = [[AR(32768 + (s * 4 + i) * 4096, 4096, F32, "gt%d_%d" % (s, i)) for i in range(4)] for s in range(2)]
            for ob in range(2):
                wga, wgb, wbr = wtile(W["ga%d" % ob]), wtile(W["gb%d" % ob]), wtile(W["br%d" % ob])
                gav, gbv = kview(wga, 8, 512), kview(wgb, 8, 512)
                brav = wbr.ap[:, 0:2048].rearrange("p (k n) -> p k n", k=4)
                brbv = wbr.ap[:, 2048:4096].rearrange("p (k n) -> p k n", k=4)
                for oi in range(4):
                    oc = ob * 4 + oi
                    os_ = slice(oi * 128, (oi + 1) * 128)
                    sa, sb_, t1, t2 = gt[oi % 2]
                    pA, pC = ps2(), ps2()
                    for t_ in range(2):
                        sl = slice(t_ * 512, (t_ + 1) * 512)
                        for kc in range(8):
                            mm(pA, pA.ap[:, sl], wga, gav[:, kc, os_], h[kc][t_], h[kc][t_].ap, kc == 0, kc == 7, kc == 7)
                    for t_ in range(2):
                        sl = slice(t_ * 512, (t_ + 1) * 512)
                        for kc in range(4):
                            mm(pC, pC.ap[:, sl], wbr, brav[:, kc, os_], og_all, ogv[:, kc, sl], kc == 0, kc == 3, kc == 3)
                    act(sa, sa.ap, pA, pA.ap, AF.Sigmoid)
                    tt(t1, t1.ap, sa, sa.ap, pC, pC.ap, ALU.mult)
                    pB, pD = ps2(), ps2()
                    for t_ in range(2):
                        sl = slice(t_ * 512, (t_ + 1) * 512)
                        for kc in range(8):
                            mm(pB, pB.ap[:, sl], wgb, gbv[:, kc, os_], h[kc][t_], h[kc][t_].ap, kc == 0, kc == 7, kc == 7)
                    for t_ in range(2):
                        sl = slice(t_ * 512, (t_ + 1) * 512)
                        for kc in range(4):
                            mm(pD, pD.ap[:, sl], wbr, brbv[:, kc, os_], ucs[kc], ucs[kc].ap[:, sl], kc == 0, kc == 3, kc == 3)
                    act(sb_, sb_.ap, pB, pB.ap, AF.Sigmoid)
                    tt(t2, t2.ap, sb_, sb_.ap, pD, pD.ap, ALU.mult)
                    tt(y[oc], y[oc].ap, t1, t1.ap, t2, t2.ap, ALU.add)
                wrel(W["ga%d" % ob])
                wrel(W["gb%d" % ob])
                wrel(W["br%d" % ob])
            for ob in range(2):
                wo = wtile(W["wo%d" % ob])
                wov = kview(wo, 8, 512)
                for oi in range(4):
                    oc = ob * 4 + oi
                    po = ps2()
                    for t_ in range(2):
                        sl = slice(t_ * 512, (t_ + 1) * 512)
                        for kc in range(8):
                            mm(po, po.ap[:, sl], wo, wov[:, kc, oi * 128:(oi + 1) * 128], y[kc], y[kc].ap[:, sl], kc == 0, kc == 7, kc == 7)
                    tt(xs[oc][hf], xs[oc][hf].ap, po, po.ap, xs[oc][hf], xs[oc][hf].ap, ALU.add)
                wrel(W["wo%d" % ob])

        def ffn(l, hf, W):
            cvl = cv[l]
            rmsnorm_h(l, hf, C_GFFN)
            actT = [AR(j * 2048, 2048, BF16, "act%d" % j) for j in range(22)]
            ZW = 1028
            Zg = [AR(45056 + i * ZW * 2, ZW * 2, BF16, "Zg%d" % i) for i in range(2)]
            Zv = [AR(45056 + (2 + i) * ZW * 2, ZW * 2, BF16, "Zv%d" % i) for i in range(2)]
            Zgh = [Tl(z.ap[:, 0:2], "Zgh") for z in Zg]
            Zvh = [Tl(z.ap[:, 0:2], "Zvh") for z in Zv]
            a1 = [[AR(53280 + (p_ * 2 + gv) * 4096, 4096, F32, "a1_%d%d" % (p_, gv)) for gv in range(2)] for p_ in range(2)]
            sgt = [AR(69664 + i * 2048, 2048, BF16, "fsg%d" % i) for i in range(2)]

            def ffn_finish(j):
                par = j % 2
                sg = sgt[par]
                ag, av = a1[par][0], a1[par][1]
                act(sg, sg.ap, ag, ag.ap, AF.Silu)
                tt(actT[j], actT[j].ap, av, av.ap, sg, sg.ap, ALU.mult)

            pending = None
            for g in range(6):
                nj = 4 if g < 5 else 2
                nco = 128 * nj
                wg_, wv_ = wtile(W["ug%d" % g]), wtile(W["uv%d" % g])
                wgv, wvv = kview(wg_, 8, nco), kview(wv_, 8, nco)
                for ji in range(nj):
                    j = g * 4 + ji
                    par = j % 2
                    zg, zv = Zg[par], Zv[par]
                    js = slice(ji * 128, (ji + 1) * 128)
                    zgh, zvh = Zgh[par], Zvh[par]
                    for (gv, z, zh, wt_, wvw, jj) in ((0, zg, zgh, wg_, wgv, j), (1, zv, zvh, wv_, wvv, 22 + j)):
                        wc = C_FW + jj * 3
                        ac = a1[par][gv]
                        if hf == 0:
                            mset(zh, zh.ap, 0.0)
                        else:
                            cp(zh, zh.ap, stash, stash.ap[:, jj, :])
                        pz = ps2()
                        for t_ in range(2):
                            sl = slice(t_ * 512, (t_ + 1) * 512)
                            for kc in range(8):
                                mm(pz, pz.ap[:, sl], wt_, wvw[:, kc, js], h[kc][t_], h[kc][t_].ap, kc == 0, kc == 7, kc == 7)
                        act(z, z.ap[:, 2:2 + T], pz, pz.ap, AF.Copy)
                        act(ac, ac.ap, pz, pz.ap, AF.Identity, bias=cvl.ap[:, C_FB + jj:C_FB + jj + 1],
                            scale=cvl.ap[:, wc + 2:wc + 3], rd=[cvl])
                    for (gv, z, jj) in ((0, zg, j), (1, zv, 22 + j)):
                        wc = C_FW + jj * 3
                        ac = a1[par][gv]
                        if hf == 0:
                            cp(stash, stash.ap[:, jj, :], z, z.ap[:, T:T + 2])
                    for k in (1, 0):
                        for (gv, z, zh, jj) in ((0, zg, zgh, j), (1, zv, zvh, 22 + j)):
                            wc = C_FW + jj * 3
                            ac = a1[par][gv]
                            stt(ac, ac.ap, z, z.ap[:, k:k + T], cvl.ap[:, wc + k:wc + k + 1], ac, ac.ap, ALU.mult, ALU.add, rd=[cvl, zh])
                    if pending is not None:
                        ffn_finish(pending)
                    pending = j
                wrel(W["ug%d" % g])
                wrel(W["uv%d" % g])
            ffn_finish(pending)
            for cb in range(4):
                w0, w1 = wtile(W["wd%d_0" % cb]), wtile(W["wd%d_1" % cb])
                w0v = w0.ap[:, 0:2816].rearrange("p (k n) -> p k n", k=11)
                w1v = w1.ap[:, 0:2816].rearrange("p (k n) -> p k n", k=11)
                for oi in range(2):
                    oc = cb * 2 + oi
                    os_ = slice(oi * 128, (oi + 1) * 128)
                    pd = ps2()
                    for t_ in range(2):
                        sl = slice(t_ * 512, (t_ + 1) * 512)
                        for kc in range(22):
                            wt_, wv_ = (w0, w0v) if kc < 11 else (w1, w1v)
                            mm(pd, pd.ap[:, sl], wt_, wv_[:, kc % 11, os_], actT[kc], actT[kc].ap[:, sl], kc == 0, kc == 21, kc == 21)
                    tt(xs[oc][hf], xs[oc][hf].ap, pd, pd.ap, xs[oc][hf], xs[oc][hf].ap, ALU.add)
                wrel(W["wd%d_0" % cb])
                wrel(W["wd%d_1" % cb])

        def ple(l, hf, W):
            pbt = AR(0, 4096, BF16, "pb")
            pbv = pbt.ap.rearrange("p (k n) -> p k n", k=2)
            P.dma("pool", lambda e: e.dma_start(out=pbv, in_=pT[l, :, hf * T:(hf + 1) * T].rearrange("(k p) n -> p k n", p=128)), out_t=pbt)

            def pp_mm(ob, oi):
                wp_ = wtile(W["pp%d" % ob])
                wpv = kview(wp_, 2, 512)
                os_ = slice(oi * 128, (oi + 1) * 128)
                pp = ps2()
                for t_ in range(2):
                    sl = slice(t_ * 512, (t_ + 1) * 512)
                    for kc in range(2):
                        mm(pp, pp.ap[:, sl], wp_, wpv[:, kc, os_], pbt, pbv[:, kc, sl], kc == 0, kc == 1, kc == 1)
                return pp

            pre_pp = {(0, 0): pp_mm(0, 0), (0, 1): pp_mm(0, 1)}
            rmsnorm_h(l, hf, C_GPLE)
            sg2 = [AR(4096 + i * 4096, 4096, F32, "psg%d" % i) for i in range(2)]
            tp2 = [AR(12288 + i * 4096, 4096, F32, "ptp%d" % i) for i in range(2)]
            for ob in range(2):
                wg_, wp_ = wtile(W["pg%d" % ob]), wtile(W["pp%d" % ob])
                wgv, wpv = kview(wg_, 8, 512), kview(wp_, 2, 512)
                for oi in range(4):
                    oc = ob * 4 + oi
                    os_ = slice(oi * 128, (oi + 1) * 128)
                    sg, tp = sg2[oi % 2], tp2[oi % 2]
                    pg = ps2()
                    for t_ in range(2):
                        sl = slice(t_ * 512, (t_ + 1) * 512)
                        for kc in range(8):
                            mm(pg, pg.ap[:, sl], wg_, wgv[:, kc, os_], h[kc][t_], h[kc][t_].ap, kc == 0, kc == 7, kc == 7)
                    pp = pre_pp.get((ob, oi)) or pp_mm(ob, oi)
                    act(sg, sg.ap, pg, pg.ap, AF.Sigmoid)
                    tt(tp, tp.ap, sg, sg.ap, pp, pp.ap, ALU.mult)
                    tt(xs[oc][hf], xs[oc][hf].ap, tp, tp.ap, xs[oc][hf], xs[oc][hf].ap, ALU.add)
                wrel(W["pg%d" % ob])
                wrel(W["pp%d" % ob])

        out_tiles = []

        def final(hf, raw):
            ost = [AR(16384 + i * 2048, 2048, F32, "ost%d" % i) for i in range(2)]
            out_tiles.extend(ost)
            l = DEPTH - 1
            if not raw:
                nsq = [AR(NSCR + i * 1024, 1024, BF16, "fsq%d" % i) for i in range(4)]
                nrs = [AR(NSCR + 4096 + i * 2048, 2048, F32, "frs%d" % i) for i in range(2)]
            n = 0
            for t_ in range(2):
                sl = slice(t_ * 512, (t_ + 1) * 512)
                if not raw:
                    ss = ps()
                    for kc in range(8):
                        sq = nsq[kc % 4]
                        act(sq, sq.ap, xs[kc][hf], xs[kc][hf].ap[:, sl], AF.Square)
                        mm(ss, ss.ap, ones, ones.ap, sq, sq.ap, kc == 0, kc == 7, True)
                    sd = nrs[t_]
                    act(sd, sd.ap, ss, ss.ap, AF.Ln, bias=EPS, scale=1.0 / D)
                    act(sd, sd.ap, sd, sd.ap, AF.Exp, scale=-0.5)
                for kc in range(8):
                    o = ost[n % 2]
                    n += 1
                    if raw:
                        cp(o, o.ap, xs[kc][hf], xs[kc][hf].ap[:, sl])
                    else:
                        stt(o, o.ap, xs[kc][hf], xs[kc][hf].ap[:, sl], cv[l].ap[:, C_GFIN + kc:C_GFIN + kc + 1], sd, sd.ap,
                            ALU.mult, ALU.mult, rd=[cv[l]])
                    dst = outT[kc * 128:(kc + 1) * 128, hf * T + t_ * 512:hf * T + (t_ + 1) * 512]
                    P.dma("sp", lambda e, o=o, dst=dst: e.dma_start(out=dst, in_=o.ap), in_t=o, sem_t=o)

        pi = 0
        for l in range(n_layers):
            st = stages_for(l)
            for hf in range(NHALF):
                W = plans[pi]
                pi += 1
                if sub is not None and hf == 1:
                    final(hf, raw=True)
                    continue
                mixer(l, hf, W)
                if "ffn" in st:
                    ffn(l, hf, W)
                if "ple" in st:
                    ple(l, hf, W)
                if l == n_layers - 1:
                    final(hf, raw=(stop is not None))
        P._wait("sp", {t.dsem: t.dcnt for t in out_tiles})
        P.run(block)
    return nc


def _cvec(inp):
    cv = np.zeros((DEPTH, 128, NCOL), np.float32)
    f = lambda a: np.asarray(a, np.float32)
    for l in range(DEPTH):
        cv[l, :, C_GMIX:C_GMIX + 8] = f(inp["g_mix"])[l].reshape(8, 128).T
        cv[l, :, C_GFFN:C_GFFN + 8] = f(inp["g_ffn"])[l].reshape(8, 128).T
        cv[l, :, C_GPLE:C_GPLE + 8] = f(inp["g_ple"])[l].reshape(8, 128).T
        cv[l, :, C_GFIN:C_GFIN + 8] = f(inp["g_final"]).reshape(8, 128).T
        cv[l, :, C_HGN] = f(inp["hg_norm_g"])[l]
        cv[l, :, C_BGLU:C_BGLU + 8] = f(inp["b_glu"])[l].reshape(8, 128).T
        cv[l, :, C_CW:C_CW + 124] = f(inp["conv_w"])[l].reshape(31, 4, 128).transpose(2, 1, 0).reshape(128, 124)
        cv[l, :, C_CB:C_CB + 4] = f(inp["conv_b"])[l].reshape(4, 128).T
        cv[l, :, C_LNG:C_LNG + 4] = f(inp["ln_g"])[l].reshape(4, 128).T
        cv[l, :, C_LNB:C_LNB + 4] = f(inp["ln_b"])[l].reshape(4, 128).T
        cv[l, :, C_FW:C_FW + 132] = f(inp["ffn_conv_w"])[l].reshape(3, 44, 128).transpose(2, 1, 0).reshape(128, 132)
        cv[l, :, C_FB:C_FB + 44] = f(inp["ffn_conv_b"])[l].reshape(44, 128).T
        cv[l, :, C_LG0:C_LG0 + 4] = f(inp["hg_lb_logits"])[0].reshape(4, 128).T
        cv[l, :, C_LG1:C_LG1 + 4] = f(inp["hg_lb_logits"])[1].reshape(4, 128).T
    return cv


def _consts():
    c = np.zeros((128, NCONST), np.float32)
    c[:, 0:128] = np.eye(128, dtype=np.float32)
    s = np.arange(128)[:, None]
    t = np.arange(128)[None, :]
    c[:, 128:256] = (t >= s).astype(np.float32)
    sm = np.ones((128, 512), np.float32)
    sm[:, 0::128] = 0.0
    c[:, 256:768] = sm
    return c


_NC_CACHE = {}


def kernel(**inputs):
    inp = {k: np.asarray(v) for k, v in inputs.items()}
    key = (DEPTH, DEBUG_STOP, DEBUG_SUB)
    if key not in _NC_CACHE:
        nl = DEPTH if DEBUG_STOP is None else DEBUG_STOP[0] + 1
        _NC_CACHE[key] = build_nc(nl, DEBUG_STOP, DEBUG_SUB)
    nc = _NC_CACHE[key]
    x = np.asarray(inp["x"], np.float32)
    p = np.asarray(inp["p"], np.float32)
    cvec = _cvec(inp)
    consts = _consts()
    shared = {k: np.ascontiguousarray(inp[k], dtype=np.float32) for k in
              ("w_in", "w_br_a", "w_br_b", "w_out", "w_up", "w_down", "w_ple_gate", "w_ple_proj")}
    in_maps = []
    for b in range(8):
        m = dict(shared)
        m["xT"] = np.ascontiguousarray(x[b].T)
        m["pT"] = np.ascontiguousarray(p[:, b].transpose(0, 2, 1))
        m["cvec"] = cvec
        m["consts"] = consts
        in_maps.append(m)
    res = run_bass_kernel_spmd(nc, in_maps, core_ids=list(range(8)))
    out = np.stack([np.ascontiguousarray(res.results[b]["outT"].T) for b in range(8)], axis=0)
    return out.astype(np.float32)
```

```python
from contextlib import ExitStack
import numpy as np
import concourse.bass as bass
import concourse.mybir as mybir
from concourse.bass_utils import run_bass_kernel_spmd

F32 = mybir.dt.float32
BF16 = mybir.dt.bfloat16
AF = mybir.ActivationFunctionType
ALU = mybir.AluOpType

D = 1024
SEQ = 2048
T = 1024
NHALF = 2
DEPTH = 2
EPS = 1e-6
DFF = 2816
NCOL = 368
C_GMIX, C_GFFN, C_GPLE, C_GFIN, C_HGN, C_BGLU, C_CW, C_CB, C_LNG, C_LNB, C_FW, C_FB, C_LG0, C_LG1 = (
    0, 8, 16, 24, 32, 33, 41, 165, 169, 173, 177, 309, 353, 357)
NCONST = 128 + 128 + 512
ARENA = 77824
NSLOT = 5
NORM_POOL_KC = (2, 5, 7)

DEBUG_STOP = None
DEBUG_SUB = None


class Tl:
    __slots__ = ("ap", "w", "r", "dsem", "dcnt", "name", "dead")

    def __init__(self, ap, name=""):
        self.ap = ap
        self.w = None
        self.r = {}
        self.dsem = None
        self.dcnt = 0
        self.name = name
        self.dead = False


class Prog:
    ENG = ("pe", "act", "dve", "pool", "sp")

    def __init__(self, nc, es):
        self.nc = nc
        self.es = es
        self.q = {e: [] for e in self.ENG}
        self.cnt = {e: 0 for e in self.ENG}
        self.sem = {e: es.enter_context(nc.semaphore("s_" + e)) for e in self.ENG}
        self.seen = {e: {} for e in self.ENG}
        self.nsem = 0

    def new_sem(self):
        self.nsem += 1
        return self.es.enter_context(self.nc.semaphore("d%d" % self.nsem))

    def _wait(self, eng, deps):
        for k, v in deps.items():
            if self.seen[eng].get(k, 0) >= v:
                continue
            self.seen[eng][k] = v
            sem = self.sem[k] if isinstance(k, str) else k
            self.q[eng].append(lambda e, sem=sem, v=v: e.wait_ge(sem, v))

    def _deps(self, eng, reads, writes):
        deps = {}

        def add(k, v):
            if v > deps.get(k, 0):
                deps[k] = v

        for t in reads:
            assert not t.dead, t.name
            if t.w:
                add(*t.w)
        for t in writes:
            assert not t.dead, t.name
            if t.w:
                add(*t.w)
            for k, v in t.r.items():
                add(k, v)
        if eng == "pe":
            deps.pop("pe", None)
        return deps

    def op(self, eng, fn, reads=(), writes=(), inc=True):
        self._wait(eng, self._deps(eng, reads, writes))
        if inc:
            self.cnt[eng] += 1
            c = self.cnt[eng]
            sem = self.sem[eng]
            self.q[eng].append(lambda e, fn=fn, sem=sem: fn(e).then_inc(sem, 1))
        else:
            c = self.cnt[eng] + 1
            self.q[eng].append(fn)
        for t in reads:
            if t.r.get(eng, 0) < c:
                t.r[eng] = c
        for t in writes:
            t.w = (eng, c)
            t.r = {}

    def dma(self, eng, fn, out_t=None, in_t=None, sem_t=None):
        reads = [in_t] if in_t is not None else []
        writes = [out_t] if out_t is not None else []
        self._wait(eng, self._deps(eng, reads, writes))
        st = out_t if out_t is not None else sem_t
        if st.dsem is None:
            st.dsem = self.new_sem()
        st.dcnt += 16
        sem, v = st.dsem, st.dcnt
        self.q[eng].append(lambda e, fn=fn, sem=sem: fn(e).then_inc(sem, 16))
        if out_t is not None:
            out_t.w = (sem, v)
            out_t.r = {}
        if in_t is not None:
            in_t.r[sem] = v

    def run(self, block):
        q = self.q

        @block.tensor
        def _(e):
            for f in q["pe"]:
                f(e)

        @block.scalar
        def _(e):
            for f in q["act"]:
                f(e)

        @block.vector
        def _(e):
            for f in q["dve"]:
                f(e)

        @block.gpsimd
        def _(e):
            for f in q["pool"]:
                f(e)

        @block.sync
        def _(e):
            for f in q["sp"]:
                f(e)


def build_nc(n_layers=DEPTH, stop=None, sub=None):
    nc = bass.Bass("TRN2", target_bir_lowering=False)
    dr = {}

    def din(name, shape):
        dr[name] = nc.dram_tensor(name, list(shape), F32, kind="ExternalInput").ap()
        return dr[name]

    xT = din("xT", [D, SEQ])
    pT = din("pT", [DEPTH, 256, SEQ])
    cvec_d = din("cvec", [DEPTH, 128, NCOL])
    const_d = din("consts", [128, NCONST])
    w_in = din("w_in", [DEPTH, D, 5120])
    w_br_a = din("w_br_a", [DEPTH, 512, D])
    w_br_b = din("w_br_b", [DEPTH, 512, D])
    w_out = din("w_out", [DEPTH, D, D])
    w_up = din("w_up", [DEPTH, D, 2 * DFF])
    w_down = din("w_down", [DEPTH, DFF, D])
    w_pg = din("w_ple_gate", [DEPTH, D, D])
    w_pp = din("w_ple_proj", [DEPTH, 256, D])
    outT = nc.dram_tensor("outT", [D, SEQ], F32, kind="ExternalOutput").ap()

    with ExitStack() as es:
        x_sb = es.enter_context(nc.sbuf_tensor("x_sb", [128, 8, SEQ], F32))
        h_sb = es.enter_context(nc.sbuf_tensor("h_sb", [128, 8, T], BF16))
        ws_sb = es.enter_context(nc.sbuf_tensor("ws_sb", [128, NSLOT, 4096], BF16))
        ar = es.enter_context(nc.sbuf_tensor("arena", [128, ARENA // 2], BF16))
        cv_sb = es.enter_context(nc.sbuf_tensor("cv_sb", [128, DEPTH, NCOL], F32))
        ident_sb = es.enter_context(nc.sbuf_tensor("ident", [128, 128], BF16))
        ones_sb = es.enter_context(nc.sbuf_tensor("ones", [128, 128], BF16))
        mask_sb = es.enter_context(nc.sbuf_tensor("mask", [128, 128], F32))
        smask_sb = es.enter_context(nc.sbuf_tensor("smask", [128, 512], F32))
        S32_sb = es.enter_context(nc.sbuf_tensor("S32", [128, 512], F32))
        uhalo_sb = es.enter_context(nc.sbuf_tensor("uhalo", [128, 4, 32], BF16))
        stash_sb = es.enter_context(nc.sbuf_tensor("stash", [128, 44, 2], BF16))
        lb_sb = es.enter_context(nc.sbuf_tensor("lb", [128, 4, 4], F32))
        lbc_sb = es.enter_context(nc.sbuf_tensor("lbc", [128, 2, 4], F32))
        hsm_sb = es.enter_context(nc.sbuf_tensor("hsm", [128, 3, 32], F32))
        lbs_sb = es.enter_context(nc.sbuf_tensor("lbs", [128, 2, 2, 4], F32))
        pss = [es.enter_context(nc.psum_tensor("ps%d" % i, [128, 1024], F32)) for i in range(4)]
        P = Prog(nc, es)
        block = es.enter_context(nc.Block())

        xs = [[Tl(x_sb[:, kc, hf * T:(hf + 1) * T], "x%d_%d" % (kc, hf)) for hf in range(NHALF)] for kc in range(8)]
        h = [[Tl(h_sb[:, kc, t_ * 512:(t_ + 1) * 512], "h%d_%d" % (kc, t_)) for t_ in range(2)] for kc in range(8)]
        slots = [Tl(ws_sb[:, i, :], "slot%d" % i) for i in range(NSLOT)]
        cv = [Tl(cv_sb[:, l, :], "cv%d" % l) for l in range(DEPTH)]
        ident = Tl(ident_sb[:], "ident")
        ones = Tl(ones_sb[:], "ones")
        mask = Tl(mask_sb[:], "mask")
        smask = Tl(smask_sb[:], "smask")
        S32 = Tl(S32_sb[:], "S32")
        uhalo = Tl(uhalo_sb[:], "uhalo")
        stash = Tl(stash_sb[:], "stash")
        lbt = Tl(lb_sb[:], "lb")
        lbc = Tl(lbc_sb[:], "lbc")
        elc = Tl(hsm_sb[:, 0, :], "elc")
        emc = Tl(hsm_sb[:, 1, :], "emc")
        mm_ = Tl(hsm_sb[:, 2, :], "m")
        lbs = Tl(lbs_sb[:], "lbs")
        ps_live = []
        ps_state = {"mode": None, "tiles": [], "rr": 0}

        def PSTL(b0, nb):
            k = b0 // 2
            ap = pss[k][:, :] if nb == 2 else pss[k][:, (b0 % 2) * 512:(b0 % 2 + 1) * 512]
            t = Tl(ap, "psum%d_%d" % (b0, nb))
            keep = []
            for (o, e_, old) in ps_live:
                if o < b0 + nb and b0 < e_:
                    old.dead = True
                    for kk, val in old.r.items():
                        if t.r.get(kk, 0) < val:
                            t.r[kk] = val
                    if old.w:
                        kk, val = old.w
                        if t.r.get(kk, 0) < val:
                            t.r[kk] = val
                else:
                    keep.append((o, e_, old))
            ps_live[:] = keep
            ps_live.append((b0, b0 + nb, t))
            return t

        def ps_mode(mode):
            if ps_state["mode"] == mode:
                return
            ps_state["mode"] = mode
            ps_state["rr"] = 0
            if mode == "single":
                ps_state["tiles"] = [PSTL(i, 1) for i in range(8)]
            else:
                ps_state["tiles"] = [PSTL(2 * i, 2) for i in range(4)]

        def ps():
            ps_mode("single")
            t = ps_state["tiles"][ps_state["rr"] % 8]
            ps_state["rr"] += 1
            return t

        def ps2():
            ps_mode("pair")
            t = ps_state["tiles"][ps_state["rr"] % 4]
            ps_state["rr"] += 1
            return t

        live = []

        def AR(off, nbytes, dtype=BF16, name=""):
            assert off % 4 == 0 and nbytes % 4 == 0 and off + nbytes <= ARENA, (name, off, nbytes)
            v = ar[:, off // 2:(off + nbytes) // 2]
            if dtype == F32:
                v = v.bitcast(F32)
            t = Tl(v, name)
            keep = []
            for (o, e_, old) in live:
                if o < off + nbytes and off < e_:
                    old.dead = True
                    for k, val in old.r.items():
                        if t.r.get(k, 0) < val:
                            t.r[k] = val
                    if old.w:
                        k, val = old.w
                        if t.r.get(k, 0) < val:
                            t.r[k] = val
                else:
                    keep.append((o, e_, old))
            live[:] = keep
            live.append((off, off + nbytes, t))
            return t

        def mm(out_t, out_ap, l_t, l_ap, r_t, r_ap, start, stop, inc):
            P.op("pe", lambda e: e.matmul(out_ap, l_ap, r_ap, start=start, stop=stop),
                 reads=[l_t, r_t], writes=[out_t], inc=inc)

        def act(out_t, out_ap, in_t, in_ap, func, bias=None, scale=None, rd=()):
            kw = {}
            if bias is not None:
                kw["bias"] = bias
            if scale is not None:
                kw["scale"] = scale
            P.op("act", lambda e: e.activation(out=out_ap, in_=in_ap, func=func, **kw),
                 reads=[in_t] + list(rd), writes=[out_t])

        def tt(out_t, out_ap, a_t, a_ap, b_t, b_ap, op, eng="dve"):
            P.op(eng, lambda e: e.tensor_tensor(out=out_ap, in0=a_ap, in1=b_ap, op=op),
                 reads=[a_t, b_t], writes=[out_t])

        def stt(out_t, out_ap, a_t, a_ap, scalar, b_t, b_ap, op0, op1, rd=()):
            P.op("dve", lambda e: e.scalar_tensor_tensor(out=out_ap, in0=a_ap, scalar=scalar, in1=b_ap, op0=op0, op1=op1),
                 reads=[a_t, b_t] + list(rd), writes=[out_t])

        def ts(out_t, out_ap, a_t, a_ap, s1, s2, op0, op1=None, rd=(), eng="dve"):
            if op1 is None:
                P.op(eng, lambda e: e.tensor_scalar(out=out_ap, in0=a_ap, scalar1=s1, scalar2=None, op0=op0),
                     reads=[a_t] + list(rd), writes=[out_t])
            else:
                P.op(eng, lambda e: e.tensor_scalar(out=out_ap, in0=a_ap, scalar1=s1, scalar2=s2, op0=op0, op1=op1),
                     reads=[a_t] + list(rd), writes=[out_t])

        def cp(out_t, out_ap, in_t, in_ap, eng="dve"):
            P.op(eng, lambda e: e.tensor_copy(out=out_ap, in_=in_ap), reads=[in_t], writes=[out_t])

        def mset(t, ap, val, eng="dve"):
            P.op(eng, lambda e: e.memset(ap, val), writes=[t])

        blocks = []
        wstate = {"emitted": 0, "released": -1}

        def wadd(fns):
            blocks.append(fns)
            return len(blocks) - 1

        def w_emit_upto(n):
            while wstate["emitted"] < min(n + 1, len(blocks)):
                i = wstate["emitted"]
                st = slots[i % NSLOT]
                for fn in blocks[i]:
                    P.dma("pool", lambda e, fn=fn, st=st: fn(e, st.ap), out_t=st)
                wstate["emitted"] += 1

        def wtile(i):
            assert i < wstate["emitted"], (i, wstate)
            assert i > wstate["released"]
            return slots[i % NSLOT]

        def wrel(i):
            assert i == wstate["released"] + 1, (i, wstate)
            wstate["released"] = i
            w_emit_upto(i + NSLOT)

        def blk_k(src, l, k, c0, ncol):
            def fn(e, sap):
                return e.dma_start(out=sap[:, 0:k * ncol].rearrange("p (k n) -> p k n", k=k),
                                   in_=src[l, :, c0:c0 + ncol].rearrange("(k p) n -> p k n", p=128))
            return fn

        def blk_br(l, c0):
            def fa(e, sap):
                return e.dma_start(out=sap[:, 0:2048].rearrange("p (k n) -> p k n", k=4),
                                   in_=w_br_a[l, :, c0:c0 + 512].rearrange("(k p) n -> p k n", p=128))

            def fb(e, sap):
                return e.dma_start(out=sap[:, 2048:4096].rearrange("p (k n) -> p k n", k=4),
                                   in_=w_br_b[l, :, c0:c0 + 512].rearrange("(k p) n -> p k n", p=128))
            return [fa, fb]

        def blk_wd(l, kh, c0):
            def fn(e, sap):
                return e.dma_start(out=sap[:, 0:2816].rearrange("p (k n) -> p k n", k=11),
                                   in_=w_down[l, kh * 1408:(kh + 1) * 1408, c0:c0 + 256].rearrange("(k p) n -> p k n", p=128))
            return fn

        def plan_pass(l, stages):
            W = {}
            if "mix" in stages:
                for nm, b in (("f", 1), ("q", 0), ("v", 2), ("og", 3), ("glua", 4), ("glub", 5)):
                    W[nm] = wadd([blk_k(w_in, l, 8, b * 512, 512)])
                for ob in range(2):
                    W["ga%d" % ob] = wadd([blk_k(w_in, l, 8, 3072 + ob * 512, 512)])
                    W["gb%d" % ob] = wadd([blk_k(w_in, l, 8, 4096 + ob * 512, 512)])
                    W["br%d" % ob] = wadd(blk_br(l, ob * 512))
                for ob in range(2):
                    W["wo%d" % ob] = wadd([blk_k(w_out, l, 8, ob * 512, 512)])
            if "ffn" in stages:
                for g in range(6):
                    nco = 512 if g < 5 else 256
                    W["ug%d" % g] = wadd([blk_k(w_up, l, 8, g * 512, nco)])
                    W["uv%d" % g] = wadd([blk_k(w_up, l, 8, DFF + g * 512, nco)])
                for cb in range(4):
                    W["wd%d_0" % cb] = wadd([blk_wd(l, 0, cb * 256)])
                    W["wd%d_1" % cb] = wadd([blk_wd(l, 1, cb * 256)])
            if "ple" in stages:
                for ob in range(2):
                    W["pg%d" % ob] = wadd([blk_k(w_pg, l, 8, ob * 512, 512)])
                    W["pp%d" % ob] = wadd([blk_k(w_pp, l, 2, ob * 512, 512)])
            return W

        def stages_for(l):
            if stop is not None and l == stop[0]:
                return {"mix": ["mix"], "ffn": ["mix", "ffn"], "ple": ["mix", "ffn", "ple"]}[stop[1]]
            return ["mix", "ffn", "ple"]

        plans = []
        for l in range(n_layers):
            for hf in range(NHALF):
                plans.append(plan_pass(l, stages_for(l)))

        for l in range(DEPTH):
            P.dma("sp", lambda e, l=l: e.dma_start(out=cv[l].ap, in_=cvec_d[l, :, :]), out_t=cv[l])
        P.dma("sp", lambda e: e.dma_start(out=mask.ap, in_=const_d[:, 128:256]), out_t=mask)
        P.dma("sp", lambda e: e.dma_start(out=smask.ap, in_=const_d[:, 256:768]), out_t=smask)
        P.dma("pool", lambda e: e.dma_start(out=ident.ap, in_=const_d[:, 0:128]), out_t=ident)
        for hf in range(NHALF):
            for kc in range(8):
                P.dma("sp", lambda e, kc=kc, hf=hf: e.dma_start(out=xs[kc][hf].ap, in_=xT[kc * 128:(kc + 1) * 128, hf * T:(hf + 1) * T]),
                      out_t=xs[kc][hf])
        w_emit_upto(1)
        mset(ones, ones.ap, 1.0)
        mset(lbc, lbc.ap[:, 0, :], 0.0)
        mset(lbc, lbc.ap[:, 1, :], 1.0)
        act(lbt, lbt.ap[:, 2, :], cv[0], cv[0].ap[:, C_LG0:C_LG0 + 4], AF.Exp)
        act(lbt, lbt.ap[:, 3, :], cv[0], cv[0].ap[:, C_LG1:C_LG1 + 4], AF.Exp)
        tt(lbt, lbt.ap[:, 0, :], lbt, lbt.ap[:, 2, :], lbt, lbt.ap[:, 3, :], ALU.add)
        P.op("dve", lambda e: e.reciprocal(out=lbt.ap[:, 0, :], in_=lbt.ap[:, 0, :]), reads=[lbt], writes=[lbt])
        tt(lbt, lbt.ap[:, 1, :], lbt, lbt.ap[:, 2, :], lbt, lbt.ap[:, 0, :], ALU.mult)
        tt(lbt, lbt.ap[:, 0, :], lbt, lbt.ap[:, 3, :], lbt, lbt.ap[:, 0, :], ALU.mult)

        ts(lbs, lbs.ap[:, 0, 0, :], lbc, lbc.ap[:, 1, :], 0.5, None, ALU.mult)
        tt(lbs, lbs.ap[:, 0, 1, :], lbs, lbs.ap[:, 0, 0, :], lbc, lbc.ap[:, 0, :], ALU.add)
        ts(lbs, lbs.ap[:, 1, 0, :], lbt, lbt.ap[:, 1, :], 0.5, None, ALU.mult)
        tt(lbs, lbs.ap[:, 1, 1, :], lbs, lbs.ap[:, 1, 0, :], lbt, lbt.ap[:, 0, :], ALU.add)

        def lb_ap(l, hd):
            return lbs.ap[:, l, 1, hd:hd + 1], lbs.ap[:, l, 0, hd:hd + 1], lbs

        NSCR = 57344

        def rmsnorm_h(l, hf, gcol):
            nsq = [AR(NSCR + i * 1024, 1024, BF16, "nsq%d" % i) for i in range(4)]
            nrs = [AR(NSCR + 4096 + i * 2048, 2048, F32, "nrs%d" % i) for i in range(2)]
            for t_ in range(2):
                sl = slice(t_ * 512, (t_ + 1) * 512)
                ss = ps()
                for kc in range(8):
                    sq = nsq[kc % 4]
                    if kc % 2 == 0:
                        act(sq, sq.ap, xs[kc][hf], xs[kc][hf].ap[:, sl], AF.Square)
                    else:
                        tt(sq, sq.ap, xs[kc][hf], xs[kc][hf].ap[:, sl], xs[kc][hf], xs[kc][hf].ap[:, sl], ALU.mult)
                    mm(ss, ss.ap, ones, ones.ap, sq, sq.ap, kc == 0, kc == 7, True)
                sd = nrs[t_]
                act(sd, sd.ap, ss, ss.ap, AF.Ln, bias=EPS, scale=1.0 / D)
                act(sd, sd.ap, sd, sd.ap, AF.Exp, scale=-0.5)
                for kc in range(8):
                    stt(h[kc][t_], h[kc][t_].ap, xs[kc][hf], xs[kc][hf].ap[:, sl], cv[l].ap[:, gcol + kc:gcol + kc + 1],
                        sd, sd.ap, ALU.mult, ALU.mult, rd=[cv[l]])

        def kview(st, k, ncol):
            return st.ap[:, 0:k * ncol].rearrange("p (k n) -> p k n", k=k)

        def mixer(l, hf, W):
            cvl = cv[l]
            og_all = AR(0, 8192, BF16, "og")
            ogv = og_all.ap.rearrange("p (h t) -> p h t", h=4)
            rmsnorm_h(l, hf, C_GMIX)
            w_emit_upto(wstate["released"] + NSLOT)
            if sub == 'norm':
                return
            QT = [AR(16384 + hd * 2048, 2048, BF16, "QT%d" % hd) for hd in range(4)]
            KT = [AR(24576 + hd * 2048, 2048, BF16, "KT%d" % hd) for hd in range(4)]
            Vtok = [AR(32768 + j * 1024, 1024, BF16, "V%d" % j) for j in range(8)]
            sogX = AR(40960, 8192, BF16, "sogX")
            sogv = sogX.ap.rearrange("p (c h t) -> p c h t", c=8, h=4)
            Ktok = [AR(49152 + i * 1024, 1024, BF16, "Ktok%d" % i) for i in range(2)]
            Sb2 = [AR(51200 + i * 1024, 1024, BF16, "Sb%d" % i) for i in range(2)]
            smT = [AR(53248 + i * 1024, 1024, BF16, "smT%d" % i) for i in range(2)]
            Sp32 = AR(55296, 2048, F32, "Sp32")
            for i in range(2):
                mset(smT[i], smT[i].ap.rearrange("p (h v) -> p h v", h=4)[64:128, :, 0:64], 0.0)
            tmp = [[AR(57344 + (s * 5 + i) * 2048, 2048, F32, "tmp%d_%d" % (s, i)) for i in range(5)] for s in range(2)]
            elc3 = elc.ap.rearrange("p (h c) -> p h c", h=4)
            emc3 = emc.ap.rearrange("p (h c) -> p h c", h=4)
            m3 = mm_.ap.rearrange("p (h c) -> p h c", h=4)

            if hf == 0:
                mset(S32, S32.ap, 0.0)
            wf, wq = wtile(W["f"]), wtile(W["q"])
            wfv, wqv = kview(wf, 8, 512), kview(wq, 8, 512)
            wv, wg = wtile(W["v"]), wtile(W["og"])
            wvv, wgv = kview(wv, 8, 512), kview(wg, 8, 512)
            def gate_ph1(hd, t_):
                lb_a, oml_a, lb_t = lb_ap(l, hd)
                A, G, B, C, Fq = tmp[(hd * 2 + t_) % 2]
                pf = ps()
                for kc in range(8):
                    mm(pf, pf.ap, wf, wfv[:, kc, hd * 128:(hd + 1) * 128], h[kc][t_], h[kc][t_].ap, kc == 0, kc == 7, kc == 7)
                pq = ps()
                for kc in range(8):
                    mm(pq, pq.ap, wq, wqv[:, kc, hd * 128:(hd + 1) * 128], h[kc][t_], h[kc][t_].ap, kc == 0, kc == 7, kc == 7)
                j = hd * 2 + t_
                pz = ps()
                for kc in range(8):
                    mm(pz, pz.ap, wg, wgv[:, kc, hd * 128:(hd + 1) * 128], h[kc][t_], h[kc][t_].ap, kc == 0, kc == 7, kc == 7)
                pv = ps()
                for kc in range(8):
                    mm(pv, pv.ap, h[kc][j // 4], h[kc][j // 4].ap[:, (j % 4) * 128:(j % 4 + 1) * 128], wv, wvv[:, kc, :], kc == 0, kc == 7, kc == 7)
                act(A, A.ap, pf, pf.ap, AF.Tanh, scale=0.5)
                act(Fq, Fq.ap, pq, pq.ap, AF.Silu)
                act(sogX, sogv[:, t_ * 4:(t_ + 1) * 4, hd, :], pz, pz.ap.rearrange("p (c t) -> p c t", c=4), AF.Silu)
                ts(A, A.ap, A, A.ap, oml_a, lb_a, ALU.mult, ALU.add, rd=[lb_t])
                cp(Vtok[j], Vtok[j].ap, pv, pv.ap)
                ts(G, G.ap, A, A.ap, -1.0, 1.0, ALU.mult, ALU.add)

            def gate_ph2(hd, t_):
                sl = slice(t_ * 512, (t_ + 1) * 512)
                A, G, B, C, Fq = tmp[(hd * 2 + t_) % 2]
                act(B, B.ap, A, A.ap, AF.Ln)
                P.op("dve", lambda e, C=C, B=B: e.tensor_tensor_scan(out=C.ap, data0=smask.ap, data1=B.ap, initial=0.0,
                                                                     op0=ALU.mult, op1=ALU.add),
                     reads=[smask, B], writes=[C])
                c0 = t_ * 4
                cp(mm_, m3[:, hd, c0:c0 + 4], C, C.ap[:, 63::128])
                act(emc, emc3[:, hd, c0:c0 + 4], mm_, m3[:, hd, c0:c0 + 4], AF.Exp)
                C3 = C.ap.rearrange("p (c t) -> p c t", c=4)
                tt(C, C3, C, C3, mm_, m3[:, hd, c0:c0 + 4].unsqueeze(2).to_broadcast([128, 4, 128]), ALU.subtract)
                act(B, B.ap, C, C.ap, AF.Exp)
                act(A, A.ap, C, C.ap, AF.Exp, scale=-1.0)
                cp(elc, elc3[:, hd, c0:c0 + 4], B, B.ap[:, 127::128])
                tt(QT[hd], QT[hd].ap[:, sl], Fq, Fq.ap, B, B.ap, ALU.mult)
                tt(KT[hd], KT[hd].ap[:, sl], G, G.ap, A, A.ap, ALU.mult)

            for hd in range(4):
                gate_ph1(hd, 0)
                gate_ph1(hd, 1)
                gate_ph2(hd, 0)
                gate_ph2(hd, 1)
            wrel(W["f"])
            wrel(W["q"])
            wrel(W["v"])
            wrel(W["og"])
            UW = 1056
            u = [AR(67584 + cc * UW * 2, UW * 2, BF16, "u%d" % cc) for cc in range(4)]
            sgt = [AR(8192 + i * 2048, 2048, F32, "sgt%d" % i) for i in range(2)]
            wa, wb = wtile(W["glua"]), wtile(W["glub"])
            wav, wbv = kview(wa, 8, 512), kview(wb, 8, 512)
            for cc in range(4):
                if hf == 0:
                    mset(u[cc], u[cc].ap[:, 0:30], 0.0)
                else:
                    cp(u[cc], u[cc].ap[:, 0:30], uhalo, uhalo.ap[:, cc, 0:30])
                for t_ in range(2):
                    sl = slice(t_ * 512, (t_ + 1) * 512)
                    pa = ps()
                    for kc in range(8):
                        mm(pa, pa.ap, wa, wav[:, kc, cc * 128:(cc + 1) * 128], h[kc][t_], h[kc][t_].ap, kc == 0, kc == 7, kc == 7)
                    pg = ps()
                    for kc in range(8):
                        mm(pg, pg.ap, wb, wbv[:, kc, cc * 128:(cc + 1) * 128], h[kc][t_], h[kc][t_].ap, kc == 0, kc == 7, kc == 7)
                    sg = sgt[(cc * 2 + t_) % 2]
                    act(sg, sg.ap, pg, pg.ap, AF.Sigmoid, bias=cvl.ap[:, C_BGLU + 4 + cc:C_BGLU + 5 + cc], rd=[cvl])
                    stt(u[cc], u[cc].ap[:, 30 + t_ * 512:30 + (t_ + 1) * 512], pa, pa.ap, cvl.ap[:, C_BGLU + cc:C_BGLU + cc + 1],
                        sg, sg.ap, ALU.add, ALU.mult, rd=[cvl])
                if hf == 0:
                    cp(uhalo, uhalo.ap[:, cc, 0:30], u[cc], u[cc].ap[:, T:T + 30])
            wrel(W["glua"])
            wrel(W["glub"])
            osq_ = [AR(57344 + i * 5120, 1024, BF16, "osq%d" % i) for i in range(2)]
            osd_ = [AR(57344 + i * 5120 + 1024, 2048, F32, "osd%d" % i) for i in range(2)]
            ot32_ = [AR(57344 + i * 5120 + 3072, 2048, F32, "ot32%d" % i) for i in range(2)]
            ps_state["mode"] = None
            ps_mode("single")
            pb = ps_state["tiles"]
            p_o = [pb[0], pb[1]]
            p_scs = [pb[2], pb[3]]
            p_kvs = [pb[4], pb[5]]
            p_ss, p_tr = pb[6], pb[7]
            p_trb = p_tr.ap.bitcast(BF16)
            S3v = S32.ap.rearrange("p (h v) -> p h v", h=4)
            Sp3v = Sp32.ap.rearrange("p (h v) -> p h v", h=4)

            def hg_pre(c):
                cs = slice(c * 128, (c + 1) * 128)
                kt, sm, p_sc, p_kv = Ktok[c % 2], smT[c % 2], p_scs[c % 2], p_kvs[c % 2]
                for hd in range(4):
                    P.op("pe", lambda e, hd=hd, cs=cs: e.transpose(p_trb[:, hd * 128:(hd + 1) * 128], KT[hd].ap[:, cs], ident.ap),
                         reads=[KT[hd], ident], writes=[p_tr], inc=(hd == 3))
                act(kt, kt.ap, p_tr, p_trb[:, 0:512], AF.Copy)
                for hd in range(4):
                    b0 = hd * 128
                    t0_ = c * 128
                    mm(p_sc, p_sc.ap[:, b0 + 64:b0 + 128], KT[hd], KT[hd].ap[:, cs], QT[hd], QT[hd].ap[:, t0_ + 64:t0_ + 128],
                       True, True, False)
                    mm(p_sc, p_sc.ap[0:64, b0:b0 + 64], KT[hd], KT[hd].ap[:, t0_:t0_ + 64], QT[hd], QT[hd].ap[:, t0_:t0_ + 64],
                       True, True, hd == 3)
                sm3 = sm.ap.rearrange("p (h v) -> p h v", h=4)
                sc3 = p_sc.ap.rearrange("p (h v) -> p h v", h=4)
                tt(sm, sm3[:, :, 64:128], p_sc, sc3[:, :, 64:128],
                   mask, mask.ap[:, 64:128].unsqueeze(1).to_broadcast([128, 4, 64]), ALU.mult)
                tt(sm, sm3[0:64, :, 0:64], p_sc, sc3[0:64, :, 0:64],
                   mask, mask.ap[0:64, 0:64].unsqueeze(1).to_broadcast([64, 4, 64]), ALU.mult)
                for hd in range(4):
                    hs = slice(hd * 128, (hd + 1) * 128)
                    mm(p_kv, p_kv.ap[:, hs], kt, kt.ap[:, hs], Vtok[c], Vtok[c].ap[:, hs], True, True, hd == 3)

            def hg_rec(c):
                cs = slice(c * 128, (c + 1) * 128)
                sm, p_kv, po, Sb = smT[c % 2], p_kvs[c % 2], p_o[c % 2], Sb2[c % 2]
                tt(Sp32, Sp3v, S32, S3v, emc, emc3[:, :, c:c + 1].to_broadcast([128, 4, 128]), ALU.mult)
                act(Sb, Sb.ap, Sp32, Sp32.ap, AF.Copy)
                tt(S32, S32.ap, p_kv, p_kv.ap, Sp32, Sp32.ap, ALU.add)
                tt(S32, S3v, S32, S3v, elc, elc3[:, :, c:c + 1].to_broadcast([128, 4, 128]), ALU.mult)
                for hd in range(4):
                    hs = slice(hd * 128, (hd + 1) * 128)
                    mm(po, po.ap[:, hs], Vtok[c], Vtok[c].ap[:, hs], sm, sm.ap[:, hs], True, False, False)
                    mm(po, po.ap[:, hs], Sb, Sb.ap[:, hs], QT[hd], QT[hd].ap[:, cs], False, True, hd == 3)
                osq, osd, ot32 = osq_[c % 2], osd_[c % 2], ot32_[c % 2]
                act(osq, osq.ap, po, po.ap, AF.Square)
                mm(p_ss, p_ss.ap, ones, ones.ap, osq, osq.ap, True, True, True)
                act(osd, osd.ap, p_ss, p_ss.ap, AF.Ln, bias=EPS, scale=1.0 / 128)
                act(osd, osd.ap, osd, osd.ap, AF.Exp, scale=-0.5)

            def hg_post(c):
                cs = slice(c * 128, (c + 1) * 128)
                po, osd, ot32 = p_o[c % 2], osd_[c % 2], ot32_[c % 2]
                stt(ot32, ot32.ap, po, po.ap, cvl.ap[:, C_HGN:C_HGN + 1], osd, osd.ap, ALU.mult, ALU.mult, rd=[cvl])
                tt(og_all, ogv[:, :, cs], ot32, ot32.ap.rearrange("p (h t) -> p h t", h=4), sogX, sogv[:, c, :, :], ALU.mult)

            hg_pre(0)
            for c in range(8):
                if c + 1 < 8:
                    hg_pre(c + 1)
                hg_rec(c)
                if c >= 1:
                    hg_post(c - 1)
            hg_post(7)

            if sub == 'hgrn':
                return
            dg = [AR(24832 + i * 7936, 7936, BF16, "dg%d" % i) for i in range(2)]
            uc32 = [AR(40704 + cc * 4096, 4096, F32, "uc32_%d" % cc) for cc in range(4)]

            dgt = [[Tl(dg[i].ap[:, k * 128:(k + 1) * 128], "dgt%d_%d" % (i, k)) for k in range(31)] for i in range(2)]
            for i in range(2):
                for k in range(31):
                    dgt[i][k].r = dict(dg[i].r)

            def build_dg(cc):
                for k in range(31):
                    tk = dgt[cc % 2][k]
                    ts(tk, tk.ap, ident, ident.ap, cvl.ap[:, C_CW + cc * 31 + k:C_CW + cc * 31 + k + 1], None, ALU.mult, rd=[cvl])

            build_dg(0)
            build_dg(1)
            for cc in range(4):
                for t_ in range(2):
                    sl = slice(t_ * 512, (t_ + 1) * 512)
                    pc = ps()
                    for k in range(31):
                        tk = dgt[cc % 2][k]
                        mm(pc, pc.ap, tk, tk.ap, u[cc], u[cc].ap[:, t_ * 512 + k:t_ * 512 + k + 512], k == 0, k == 30, k == 30)
                    act(uc32[cc], uc32[cc].ap[:, sl], pc, pc.ap, AF.Identity, bias=cvl.ap[:, C_CB + cc:C_CB + cc + 1], rd=[cvl])
                if cc + 2 < 4:
                    build_dg(cc + 2)
            for i in range(2):
                for tk in dgt[i]:
                    for kk, val in list(tk.r.items()) + ([tk.w] if tk.w else []):
                        if dg[i].r.get(kk, 0) < val:
                            dg[i].r[kk] = val
            ucs = [AR(8192 + cc * 2048, 2048, BF16, "ucs%d" % cc) for cc in range(4)]
            ucb2 = [[AR(57088 + t_ * 8192 + cc * 1024, 1024, BF16, "ucb%d_%d" % (t_, cc)) for cc in range(4)] for t_ in range(2)]
            sqb2 = [[AR(61184 + t_ * 8192 + cc * 1024, 1024, BF16, "sqb%d_%d" % (t_, cc)) for cc in range(4)] for t_ in range(2)]
            mt2 = [AR(16384 + t_ * 4096, 2048, F32, "mt%d" % t_) for t_ in range(2)]
            vt2 = [AR(18432 + t_ * 4096, 2048, F32, "vt%d" % t_) for t_ in range(2)]
            lnp = []
            for t_ in range(2):
                sl = slice(t_ * 512, (t_ + 1) * 512)
                ucb, sqb = ucb2[t_], sqb2[t_]
                for cc in range(4):
                    cp(ucb[cc], ucb[cc].ap, uc32[cc], uc32[cc].ap[:, sl])
                    act(sqb[cc], sqb[cc].ap, uc32[cc], uc32[cc].ap[:, sl], AF.Square)
                ps_mode("single")
                p1 = ps_state["tiles"][6 - 2 * t_]
                for cc in range(4):
                    mm(p1, p1.ap, ones, ones.ap, ucb[cc], ucb[cc].ap, cc == 0, cc == 3, cc == 3)
                p2 = ps_state["tiles"][7 - 2 * t_]
                for cc in range(4):
                    mm(p2, p2.ap, ones, ones.ap, sqb[cc], sqb[cc].ap, cc == 0, cc == 3, cc == 3)
                lnp.append((p1, p2))
            for t_ in range(2):
                sl = slice(t_ * 512, (t_ + 1) * 512)
                p1, p2 = lnp[t_]
                mt, vt = mt2[t_], vt2[t_]
                act(mt, mt.ap, p1, p1.ap, AF.Identity, scale=1.0 / 512)
                tt(vt, vt.ap, mt, mt.ap, mt, mt.ap, ALU.mult)
                stt(vt, vt.ap, p2, p2.ap, 1.0 / 512, vt, vt.ap, ALU.mult, ALU.subtract)
                act(vt, vt.ap, vt, vt.ap, AF.Ln, bias=EPS)
                act(vt, vt.ap, vt, vt.ap, AF.Exp, scale=-0.5)
                for cc in range(4):
                    tt(uc32[cc], uc32[cc].ap[:, sl], uc32[cc], uc32[cc].ap[:, sl], mt, mt.ap, ALU.subtract)
                    tt(uc32[cc], uc32[cc].ap[:, sl], uc32[cc], uc32[cc].ap[:, sl], vt, vt.ap, ALU.mult)
                    act(ucs[cc], ucs[cc].ap[:, sl], uc32[cc], uc32[cc].ap[:, sl], AF.Silu,
                        bias=cvl.ap[:, C_LNB + cc:C_LNB + cc + 1], scale=cvl.ap[:, C_LNG + cc:C_LNG + cc + 1], rd=[cvl])

            if sub == 'conv':
                return
            y = [AR(16384 + oc * 2048, 2048, BF16, "y%d" % oc) for oc in range(8)]
            gt = [[AR(32768 + (s * 4 + i) * 4096, 4096, F32, "gt%d_%d" % (s, i)) for i in range(4)] for s in range(2)]
            for ob in range(2):
                wga, wgb, wbr = wtile(W["ga%d" % ob]), wtile(W["gb%d" % ob]), wtile(W["br%d" % ob])
                gav, gbv = kview(wga, 8, 512), kview(wgb, 8, 512)
                brav = wbr.ap[:, 0:2048].rearrange("p (k n) -> p k n", k=4)
                brbv = wbr.ap[:, 2048:4096].rearrange("p (k n) -> p k n", k=4)
                for oi in range(4):
                    oc = ob * 4 + oi
                    os_ = slice(oi * 128, (oi + 1) * 128)
                    sa, sb_, t1, t2 = gt[oi % 2]
                    pA, pC = ps2(), ps2()
                    for t_ in range(2):
                        sl = slice(t_ * 512, (t_ + 1) * 512)
                        for kc in range(8):
                            mm(pA, pA.ap[:, sl], wga, gav[:, kc, os_], h[kc][t_], h[kc][t_].ap, kc == 0, kc == 7, kc == 7)
                    for t_ in range(2):
                        sl = slice(t_ * 512, (t_ + 1) * 512)
                        for kc in range(4):
                            mm(pC, pC.ap[:, sl], wbr, brav[:, kc, os_], og_all, ogv[:, kc, sl], kc == 0, kc == 3, kc == 3)
                    act(sa, sa.ap, pA, pA.ap, AF.Sigmoid)
                    tt(t1, t1.ap, sa, sa.ap, pC, pC.ap, ALU.mult)
                    pB, pD = ps2(), ps2()
                    for t_ in range(2):
                        sl = slice(t_ * 512, (t_ + 1) * 512)
                        for kc in range(8):
                            mm(pB, pB.ap[:, sl], wgb, gbv[:, kc, os_], h[kc][t_], h[kc][t_].ap, kc == 0, kc == 7, kc == 7)
                    for t_ in range(2):
                        sl = slice(t_ * 512, (t_ + 1) * 512)
                        for kc in range(4):
                            mm(pD, pD.ap[:, sl], wbr, brbv[:, kc, os_], ucs[kc], ucs[kc].ap[:, sl], kc == 0, kc == 3, kc == 3)
                    act(sb_, sb_.ap, pB, pB.ap, AF.Sigmoid)
                    tt(t2, t2.ap, sb_, sb_.ap, pD, pD.ap, ALU.mult)
                    tt(y[oc], y[oc].ap, t1, t1.ap, t2, t2.ap, ALU.add)
                wrel(W["ga%d" % ob])
                wrel(W["gb%d" % ob])
                wrel(W["br%d" % ob])
            for ob in range(2):
                wo = wtile(W["wo%d" % ob])
                wov = kview(wo, 8, 512)
                for oi in range(4):
                    oc = ob * 4 + oi
                    po = ps2()
                    for t_ in range(2):
                        sl = slice(t_ * 512, (t_ + 1) * 512)
                        for kc in range(8):
                            mm(po, po.ap[:, sl], wo, wov[:, kc, oi * 128:(oi + 1) * 128], y[kc], y[kc].ap[:, sl], kc == 0, kc == 7, kc == 7)
                    tt(xs[oc][hf], xs[oc][hf].ap, po, po.ap, xs[oc][hf], xs[oc][hf].ap, ALU.add)
                wrel(W["wo%d" % ob])

        def ffn(l, hf, W):
            cvl = cv[l]
            rmsnorm_h(l, hf, C_GFFN)
            actT = [AR(j * 2048, 2048, BF16, "act%d" % j) for j in range(22)]
            ZW = 1028
            Zg = [AR(45056 + i * ZW * 2, ZW * 2, BF16, "Zg%d" % i) for i in range(2)]
            Zv = [AR(45056 + (2 + i) * ZW * 2, ZW * 2, BF16, "Zv%d" % i) for i in range(2)]
            Zgh = [Tl(z.ap[:, 0:2], "Zgh") for z in Zg]
            Zvh = [Tl(z.ap[:, 0:2], "Zvh") for z in Zv]
            a1 = [[AR(53280 + (p_ * 2 + gv) * 4096, 4096, F32, "a1_%d%d" % (p_, gv)) for gv in range(2)] for p_ in range(2)]
            sgt = [AR(69664 + i * 2048, 2048, BF16, "fsg%d" % i) for i in range(2)]

            def ffn_finish(j):
                par = j % 2
                sg = sgt[par]
                ag, av = a1[par][0], a1[par][1]
                act(sg, sg.ap, ag, ag.ap, AF.Silu)
                tt(actT[j], actT[j].ap, av, av.ap, sg, sg.ap, ALU.mult)

            pending = None
            for g in range(6):
                nj = 4 if g < 5 else 2
                nco = 128 * nj
                wg_, wv_ = wtile(W["ug%d" % g]), wtile(W["uv%d" % g])
                wgv, wvv = kview(wg_, 8, nco), kview(wv_, 8, nco)
                for ji in range(nj):
                    j = g * 4 + ji
                    par = j % 2
                    zg, zv = Zg[par], Zv[par]
                    js = slice(ji * 128, (ji + 1) * 128)
                    zgh, zvh = Zgh[par], Zvh[par]
                    for (gv, z, zh, wt_, wvw, jj) in ((0, zg, zgh, wg_, wgv, j), (1, zv, zvh, wv_, wvv, 22 + j)):
                        wc = C_FW + jj * 3
                        ac = a1[par][gv]
                        if hf == 0:
                            mset(zh, zh.ap, 0.0)
                        else:
                            cp(zh, zh.ap, stash, stash.ap[:, jj, :])
                        pz = ps2()
                        for t_ in range(2):
                            sl = slice(t_ * 512, (t_ + 1) * 512)
                            for kc in range(8):
                                mm(pz, pz.ap[:, sl], wt_, wvw[:, kc, js], h[kc][t_], h[kc][t_].ap, kc == 0, kc == 7, kc == 7)
                        act(z, z.ap[:, 2:2 + T], pz, pz.ap, AF.Copy)
                        act(ac, ac.ap, pz, pz.ap, AF.Identity, bias=cvl.ap[:, C_FB + jj:C_FB + jj + 1],
                            scale=cvl.ap[:, wc + 2:wc + 3], rd=[cvl])
                    for (gv, z, jj) in ((0, zg, j), (1, zv, 22 + j)):
                        wc = C_FW + jj * 3
                        ac = a1[par][gv]
                        if hf == 0:
                            cp(stash, stash.ap[:, jj, :], z, z.ap[:, T:T + 2])
                    for k in (1, 0):
                        for (gv, z, zh, jj) in ((0, zg, zgh, j), (1, zv, zvh, 22 + j)):
                            wc = C_FW + jj * 3
                            ac = a1[par][gv]
                            stt(ac, ac.ap, z, z.ap[:, k:k + T], cvl.ap[:, wc + k:wc + k + 1], ac, ac.ap, ALU.mult, ALU.add, rd=[cvl, zh])
                    if pending is not None:
                        ffn_finish(pending)
                    pending = j
                wrel(W["ug%d" % g])
                wrel(W["uv%d" % g])
            ffn_finish(pending)
            for cb in range(4):
                w0, w1 = wtile(W["wd%d_0" % cb]), wtile(W["wd%d_1" % cb])
                w0v = w0.ap[:, 0:2816].rearrange("p (k n) -> p k n", k=11)
                w1v = w1.ap[:, 0:2816].rearrange("p (k n) -> p k n", k=11)
                for oi in range(2):
                    oc = cb * 2 + oi
                    os_ = slice(oi * 128, (oi + 1) * 128)
                    pd = ps2()
                    for t_ in range(2):
                        sl = slice(t_ * 512, (t_ + 1) * 512)
                        for kc in range(22):
                            wt_, wv_ = (w0, w0v) if kc < 11 else (w1, w1v)
                            mm(pd, pd.ap[:, sl], wt_, wv_[:, kc % 11, os_], actT[kc], actT[kc].ap[:, sl], kc == 0, kc == 21, kc == 21)
                    tt(xs[oc][hf], xs[oc][hf].ap, pd, pd.ap, xs[oc][hf], xs[oc][hf].ap, ALU.add)
                wrel(W["wd%d_0" % cb])
                wrel(W["wd%d_1" % cb])

        def ple(l, hf, W):
            pbt = AR(0, 4096, BF16, "pb")
            pbv = pbt.ap.rearrange("p (k n) -> p k n", k=2)
            P.dma("pool", lambda e: e.dma_start(out=pbv, in_=pT[l, :, hf * T:(hf + 1) * T].rearrange("(k p) n -> p k n", p=128)), out_t=pbt)
            rmsnorm_h(l, hf, C_GPLE)
            sg2 = [AR(4096 + i * 4096, 4096, F32, "psg%d" % i) for i in range(2)]
            tp2 = [AR(12288 + i * 4096, 4096, F32, "ptp%d" % i) for i in range(2)]
            for ob in range(2):
                wg_, wp_ = wtile(W["pg%d" % ob]), wtile(W["pp%d" % ob])
                wgv, wpv = kview(wg_, 8, 512), kview(wp_, 2, 512)
                for oi in range(4):
                    oc = ob * 4 + oi
                    os_ = slice(oi * 128, (oi + 1) * 128)
                    sg, tp = sg2[oi % 2], tp2[oi % 2]
                    pg = ps2()
                    for t_ in range(2):
                        sl = slice(t_ * 512, (t_ + 1) * 512)
                        for kc in range(8):
                            mm(pg, pg.ap[:, sl], wg_, wgv[:, kc, os_], h[kc][t_], h[kc][t_].ap, kc == 0, kc == 7, kc == 7)
                    pp = ps2()
                    for t_ in range(2):
                        sl = slice(t_ * 512, (t_ + 1) * 512)
                        for kc in range(2):
                            mm(pp, pp.ap[:, sl], wp_, wpv[:, kc, os_], pbt, pbv[:, kc, sl], kc == 0, kc == 1, kc == 1)
                    act(sg, sg.ap, pg, pg.ap, AF.Sigmoid)
                    tt(tp, tp.ap, sg, sg.ap, pp, pp.ap, ALU.mult)
                    tt(xs[oc][hf], xs[oc][hf].ap, tp, tp.ap, xs[oc][hf], xs[oc][hf].ap, ALU.add)
                wrel(W["pg%d" % ob])
                wrel(W["pp%d" % ob])

        out_tiles = []

        def final(hf, raw):
            ost = [AR(16384 + i * 2048, 2048, F32, "ost%d" % i) for i in range(2)]
            out_tiles.extend(ost)
            l = DEPTH - 1
            if not raw:
                nsq = [AR(NSCR + i * 1024, 1024, BF16, "fsq%d" % i) for i in range(4)]
                nrs = [AR(NSCR + 4096 + i * 2048, 2048, F32, "frs%d" % i) for i in range(2)]
            n = 0
            for t_ in range(2):
                sl = slice(t_ * 512, (t_ + 1) * 512)
                if not raw:
                    ss = ps()
                    for kc in range(8):
                        sq = nsq[kc % 4]
                        act(sq, sq.ap, xs[kc][hf], xs[kc][hf].ap[:, sl], AF.Square)
                        mm(ss, ss.ap, ones, ones.ap, sq, sq.ap, kc == 0, kc == 7, True)
                    sd = nrs[t_]
                    act(sd, sd.ap, ss, ss.ap, AF.Ln, bias=EPS, scale=1.0 / D)
                    act(sd, sd.ap, sd, sd.ap, AF.Exp, scale=-0.5)
                for kc in range(8):
                    o = ost[n % 2]
                    n += 1
                    if raw:
                        cp(o, o.ap, xs[kc][hf], xs[kc][hf].ap[:, sl])
                    else:
                        stt(o, o.ap, xs[kc][hf], xs[kc][hf].ap[:, sl], cv[l].ap[:, C_GFIN + kc:C_GFIN + kc + 1], sd, sd.ap,
                            ALU.mult, ALU.mult, rd=[cv[l]])
                    dst = outT[kc * 128:(kc + 1) * 128, hf * T + t_ * 512:hf * T + (t_ + 1) * 512]
                    P.dma("sp", lambda e, o=o, dst=dst: e.dma_start(out=dst, in_=o.ap), in_t=o, sem_t=o)

        pi = 0
        for l in range(n_layers):
            st = stages_for(l)
            for hf in range(NHALF):
                W = plans[pi]
                pi += 1
                if sub is not None and hf == 1:
                    final(hf, raw=True)
                    continue
                mixer(l, hf, W)
                if "ffn" in st:
                    ffn(l, hf, W)
                if "ple" in st:
                    ple(l, hf, W)
                if l == n_layers - 1:
                    final(hf, raw=(stop is not None))
        P._wait("sp", {t.dsem: t.dcnt for t in out_tiles})
        P.run(block)
    return nc


def _cvec(inp):
    cv = np.zeros((DEPTH, 128, NCOL), np.float32)
    f = lambda a: np.asarray(a, np.float32)
    for l in range(DEPTH):
        cv[l, :, C_GMIX:C_GMIX + 8] = f(inp["g_mix"])[l].reshape(8, 128).T
        cv[l, :, C_GFFN:C_GFFN + 8] = f(inp["g_ffn"])[l].reshape(8, 128).T
        cv[l, :, C_GPLE:C_GPLE + 8] = f(inp["g_ple"])[l].reshape(8, 128).T
        cv[l, :, C_GFIN:C_GFIN + 8] = f(inp["g_final"]).reshape(8, 128).T
        cv[l, :, C_HGN] = f(inp["hg_norm_g"])[l]
        cv[l, :, C_BGLU:C_BGLU + 8] = f(inp["b_glu"])[l].reshape(8, 128).T
        cv[l, :, C_CW:C_CW + 124] = f(inp["conv_w"])[l].reshape(31, 4, 128).transpose(2, 1, 0).reshape(128, 124)
        cv[l, :, C_CB:C_CB + 4] = f(inp["conv_b"])[l].reshape(4, 128).T
        cv[l, :, C_LNG:C_LNG + 4] = f(inp["ln_g"])[l].reshape(4, 128).T
        cv[l, :, C_LNB:C_LNB + 4] = f(inp["ln_b"])[l].reshape(4, 128).T
        cv[l, :, C_FW:C_FW + 132] = f(inp["ffn_conv_w"])[l].reshape(3, 44, 128).transpose(2, 1, 0).reshape(128, 132)
        cv[l, :, C_FB:C_FB + 44] = f(inp["ffn_conv_b"])[l].reshape(44, 128).T
        cv[l, :, C_LG0:C_LG0 + 4] = f(inp["hg_lb_logits"])[0].reshape(4, 128).T
        cv[l, :, C_LG1:C_LG1 + 4] = f(inp["hg_lb_logits"])[1].reshape(4, 128).T
    return cv


def _consts():
    c = np.zeros((128, NCONST), np.float32)
    c[:, 0:128] = np.eye(128, dtype=np.float32)
    s = np.arange(128)[:, None]
    t = np.arange(128)[None, :]
    c[:, 128:256] = (t >= s).astype(np.float32)
    sm = np.ones((128, 512), np.float32)
    sm[:, 0::128] = 0.0
    c[:, 256:768] = sm
    return c


_NC_CACHE = {}


def kernel(**inputs):
    inp = {k: np.asarray(v) for k, v in inputs.items()}
    key = (DEPTH, DEBUG_STOP, DEBUG_SUB)
    if key not in _NC_CACHE:
        nl = DEPTH if DEBUG_STOP is None else DEBUG_STOP[0] + 1
        _NC_CACHE[key] = build_nc(nl, DEBUG_STOP, DEBUG_SUB)
    nc = _NC_CACHE[key]
    x = np.asarray(inp["x"], np.float32)
    p = np.asarray(inp["p"], np.float32)
    cvec = _cvec(inp)
    consts = _consts()
    shared = {k: np.ascontiguousarray(inp[k], dtype=np.float32) for k in
              ("w_in", "w_br_a", "w_br_b", "w_out", "w_up", "w_down", "w_ple_gate", "w_ple_proj")}
    in_maps = []
    for b in range(8):
        m = dict(shared)
        m["xT"] = np.ascontiguousarray(x[b].T)
        m["pT"] = np.ascontiguousarray(p[:, b].transpose(0, 2, 1))
        m["cvec"] = cvec
        m["consts"] = consts
        in_maps.append(m)
    res = run_bass_kernel_spmd(nc, in_maps, core_ids=list(range(8)))
    out = np.stack([np.ascontiguousarray(res.results[b]["outT"].T) for b in range(8)], axis=0)
    return out.astype(np.float32)
```
